# Optimizing a Trainium2 kernel written in Bass

```python
import math
import jax, jax.numpy as jnp
from jax import lax
import numpy as np

D_MODEL = 2048
BATCH = 4
SEQ = 4096
DEPTH = 1

D_MIX = D_MODEL
ATTN_WIDTH = D_MIX // 2
SSM_WIDTH = D_MIX - ATTN_WIDTH
ATTN_HEAD_DIM = 64
ATTN_VALUE_DIM = 2 * ATTN_HEAD_DIM
N_ATTN_HEADS = ATTN_WIDTH // ATTN_VALUE_DIM
QK_WIDTH = N_ATTN_HEADS * 2 * ATTN_HEAD_DIM
Q_BLOCK = 128
SUBLN_EPS = 1e-5
SSM_GROUP = 16
N_SSM_GROUPS = SSM_WIDTH // SSM_GROUP
SSM_STATE = 64
DT_MIN = 1e-3
DT_MAX = 1e-1
D_IN_PROJ = 2 * QK_WIDTH + ATTN_WIDTH + SSM_WIDTH
D_FF = 5632
NORM_EPS = 1e-6

kernel_name = "hymba_diffattn_s5_macaron"


def rmsnorm(x, g):
    xf = x.astype(jnp.float32)
    xf = xf * lax.rsqrt(jnp.mean(xf * xf, axis=-1, keepdims=True) + NORM_EPS)
    return (xf * g.astype(jnp.float32)).astype(x.dtype)


def swiglu(h, w_gate, w_up, w_down):
    return (jax.nn.silu(h @ w_gate) * (h @ w_up)) @ w_down


def alibi_slopes(n_heads):
    return jnp.exp2(-8.0 * jnp.arange(1, n_heads + 1, dtype=jnp.float32) / n_heads)


def diff_attention(q, k, v, lam, subln_g, lambda_init):
    b, s, h, _, dh = q.shape
    dv = v.shape[-1]
    n_blocks = s // Q_BLOCK
    q_blocks = jnp.moveaxis(q.reshape(b, n_blocks, Q_BLOCK, h, 2, dh), 1, 0)
    slopes = alibi_slopes(h)
    key_pos = jnp.arange(s)
    scale = dh ** -0.5

    def attend_block(args):
        q_blk, blk = args
        q_pos = blk * Q_BLOCK + jnp.arange(Q_BLOCK)
        scores = jnp.einsum('bqhcd,bkhcd->bhcqk', q_blk, k).astype(jnp.float32) * scale
        dist = (q_pos[:, None] - key_pos[None, :]).astype(jnp.float32)
        scores = scores - slopes[None, :, None, None, None] * dist
        scores = jnp.where(key_pos[None, :] <= q_pos[:, None], scores, -jnp.inf)
        probs = jax.nn.softmax(scores, axis=-1)
        diff = probs[:, :, 0] - lam * probs[:, :, 1]
        return jnp.einsum('bhqk,bkhd->bqhd', diff.astype(v.dtype), v)

    out = lax.map(attend_block, (q_blocks, jnp.arange(n_blocks)))
    out = jnp.moveaxis(out, 0, 1).reshape(b, s, h, dv)
    of = out.astype(jnp.float32)
    of = of * lax.rsqrt(jnp.mean(of * of, axis=-1, keepdims=True) + SUBLN_EPS) * subln_g.astype(jnp.float32)
    of = of * (1.0 - lambda_init)
    return of.astype(v.dtype).reshape(b, s, h * dv)


def ssm_combine(left, right):
    a_re_l, a_im_l, b_re_l, b_im_l = left
    a_re_r, a_im_r, b_re_r, b_im_r = right
    a_re = a_re_r * a_re_l - a_im_r * a_im_l
    a_im = a_re_r * a_im_l + a_im_r * a_re_l
    b_re = a_re_r * b_re_l - a_im_r * b_im_l + b_re_r
    b_im = a_re_r * b_im_l + a_im_r * b_re_l + b_im_r
    return (a_re, a_im, b_re, b_im)


def s5_mixer(u, lam_re, lam_im, log_dt, b_re, b_im, c_re, c_im, d_skip, glu_w, glu_b):
    f32 = jnp.float32
    bsz, s, _ = u.shape
    ug = u.reshape(bsz, s, N_SSM_GROUPS, SSM_GROUP).astype(f32)
    dt = jnp.exp(log_dt.astype(f32))[:, None]
    lr = lam_re.astype(f32)
    li = lam_im.astype(f32)
    mag = jnp.exp(lr * dt)
    ab_re = mag * jnp.cos(li * dt)
    ab_im = mag * jnp.sin(li * dt)
    den = lr * lr + li * li
    f_re = ((ab_re - 1.0) * lr + ab_im * li) / den
    f_im = (ab_im * lr - (ab_re - 1.0) * li) / den
    br = b_re.astype(f32)
    bi = b_im.astype(f32)
    bb_re = f_re[..., None] * br - f_im[..., None] * bi
    bb_im = f_re[..., None] * bi + f_im[..., None] * br
    bu_re = jnp.einsum('bsgh,gph->bsgp', ug, bb_re)
    bu_im = jnp.einsum('bsgh,gph->bsgp', ug, bb_im)
    a_re = jnp.broadcast_to(ab_re[None, None], (1, s, N_SSM_GROUPS, SSM_STATE))
    a_im = jnp.broadcast_to(ab_im[None, None], (1, s, N_SSM_GROUPS, SSM_STATE))
    _, _, x_re, x_im = lax.associative_scan(ssm_combine, (a_re, a_im, bu_re, bu_im), axis=1)
    y = (jnp.einsum('ghp,bsgp->bsgh', c_re.astype(f32), x_re)
         - jnp.einsum('ghp,bsgp->bsgh', c_im.astype(f32), x_im)
         + d_skip.astype(f32).reshape(N_SSM_GROUPS, SSM_GROUP) * ug)
    g = jax.nn.gelu(y.reshape(bsz, s, SSM_WIDTH)).astype(u.dtype)
    return g * jax.nn.sigmoid(g @ glu_w + glu_b)


def setup_inputs(seed: int = 0) -> dict:
    key = jax.random.key(seed)
    ks = jax.random.split(key, 28)
    f32 = jnp.float32
    L = DEPTH
    G, P, HG = N_SSM_GROUPS, SSM_STATE, SSM_GROUP

    def normal(k, shape, scale):
        return jax.random.normal(k, shape, f32) * scale

    def gain(k, shape):
        return 1.0 + 0.02 * jax.random.normal(k, shape, f32)

    n_idx = jnp.arange(P, dtype=f32)
    return {
        "x": normal(ks[0], (BATCH, SEQ, D_MODEL), 1.0),
        "ffn1_norm": gain(ks[1], (L, D_MODEL)),
        "ffn1_w_gate": normal(ks[2], (L, D_MODEL, D_FF), D_MODEL ** -0.5),
        "ffn1_w_up": normal(ks[3], (L, D_MODEL, D_FF), D_MODEL ** -0.5),
        "ffn1_w_down": normal(ks[4], (L, D_FF, D_MODEL), D_FF ** -0.5),
        "mix_norm": gain(ks[5], (L, D_MODEL)),
        "w_in": normal(ks[6], (L, D_MODEL, D_IN_PROJ), D_MODEL ** -0.5),
        "lambda_q1": normal(ks[7], (L, ATTN_HEAD_DIM), 0.1),
        "lambda_k1": normal(ks[8], (L, ATTN_HEAD_DIM), 0.1),
        "lambda_q2": normal(ks[9], (L, ATTN_HEAD_DIM), 0.1),
        "lambda_k2": normal(ks[10], (L, ATTN_HEAD_DIM), 0.1),
        "attn_subln": gain(ks[11], (L, ATTN_VALUE_DIM)),
        "ssm_lambda_re": -0.5 + normal(ks[12], (L, G, P), 0.01),
        "ssm_lambda_im": math.pi * n_idx + normal(ks[13], (L, G, P), 0.01),
        "ssm_log_dt": jax.random.uniform(ks[14], (L, G), f32, math.log(DT_MIN), math.log(DT_MAX)),
        "ssm_b_re": normal(ks[15], (L, G, P, HG), (2 * HG) ** -0.5),
        "ssm_b_im": normal(ks[16], (L, G, P, HG), (2 * HG) ** -0.5),
        "ssm_c_re": normal(ks[17], (L, G, HG, P), P ** -0.5),
        "ssm_c_im": normal(ks[18], (L, G, HG, P), P ** -0.5),
        "ssm_d": normal(ks[19], (L, SSM_WIDTH), 0.5),
        "ssm_glu_w": normal(ks[20], (L, SSM_WIDTH, SSM_WIDTH), SSM_WIDTH ** -0.5),
        "ssm_glu_b": normal(ks[21], (L, SSM_WIDTH), 0.01),
        "w_out": normal(ks[22], (L, D_MIX, D_MODEL), D_MIX ** -0.5),
        "ffn2_norm": gain(ks[23], (L, D_MODEL)),
        "ffn2_w_gate": normal(ks[24], (L, D_MODEL, D_FF), D_MODEL ** -0.5),
        "ffn2_w_up": normal(ks[25], (L, D_MODEL, D_FF), D_MODEL ** -0.5),
        "ffn2_w_down": normal(ks[26], (L, D_FF, D_MODEL), D_FF ** -0.5),
        "final_norm": gain(ks[27], (D_MODEL,)),
    }


def reference(x, ffn1_norm, ffn1_w_gate, ffn1_w_up, ffn1_w_down, mix_norm, w_in,
              lambda_q1, lambda_k1, lambda_q2, lambda_k2, attn_subln,
              ssm_lambda_re, ssm_lambda_im, ssm_log_dt, ssm_b_re, ssm_b_im, ssm_c_re, ssm_c_im,
              ssm_d, ssm_glu_w, ssm_glu_b, w_out,
              ffn2_norm, ffn2_w_gate, ffn2_w_up, ffn2_w_down, final_norm):
    bsz, s, _ = x.shape
    for l in range(DEPTH):
        x = x + 0.5 * swiglu(rmsnorm(x, ffn1_norm[l]), ffn1_w_gate[l], ffn1_w_up[l], ffn1_w_down[l])
        h = rmsnorm(x, mix_norm[l])
        proj = h @ w_in[l]
        q, k, v, u = jnp.split(proj, [QK_WIDTH, 2 * QK_WIDTH, 2 * QK_WIDTH + ATTN_WIDTH], axis=-1)
        q = q.reshape(bsz, s, N_ATTN_HEADS, 2, ATTN_HEAD_DIM)
        k = k.reshape(bsz, s, N_ATTN_HEADS, 2, ATTN_HEAD_DIM)
        v = v.reshape(bsz, s, N_ATTN_HEADS, ATTN_VALUE_DIM)
        lambda_init = 0.8 - 0.6 * math.exp(-0.3 * l)
        lam = (jnp.exp(jnp.sum(lambda_q1[l].astype(jnp.float32) * lambda_k1[l].astype(jnp.float32)))
               - jnp.exp(jnp.sum(lambda_q2[l].astype(jnp.float32) * lambda_k2[l].astype(jnp.float32)))
               + lambda_init)
        attn_out = diff_attention(q, k, v, lam, attn_subln[l], lambda_init)
        ssm_out = s5_mixer(u, ssm_lambda_re[l], ssm_lambda_im[l], ssm_log_dt[l],
                           ssm_b_re[l], ssm_b_im[l], ssm_c_re[l], ssm_c_im[l],
                           ssm_d[l], ssm_glu_w[l], ssm_glu_b[l])
        x = x + jnp.concatenate([attn_out, ssm_out.astype(attn_out.dtype)], axis=-1) @ w_out[l]
        x = x + 0.5 * swiglu(rmsnorm(x, ffn2_norm[l]), ffn2_w_gate[l], ffn2_w_up[l], ffn2_w_down[l])
    return rmsnorm(x, final_norm)
```

```python
import os
import math
import contextlib
import numpy as np
import ml_dtypes
import concourse.bass as bass
import concourse.mybir as mybir
from concourse.bass_utils import run_bass_kernel_spmd

F32 = mybir.dt.float32
BF16 = mybir.dt.bfloat16
I32 = mybir.dt.int32
AF = mybir.ActivationFunctionType
ALU = mybir.AluOpType
AX = mybir.AxisListType
ND = 12
PI = math.pi
D = 2048
DFF = 5632
NEG = -30000.0
DBG = os.environ.get("KDBG", "")
PHASES = os.environ.get("KPH", "ASTC")


class Buf:
    __slots__ = ("w", "r")

    def __init__(self):
        self.w = None
        self.r = {}


class T:
    def __init__(self, t, nb=1):
        self.t = t
        self.b = Buf()
        self.bs = [Buf() for _ in range(nb)] if nb > 1 else [self.b]


def _bufs(lst):
    out = []
    for x in lst:
        if isinstance(x, T):
            out.extend(x.bs)
        elif isinstance(x, (list, tuple)):
            out.extend(_bufs(x))
        else:
            out.append(x)
    return out


class KB:
    def __init__(self, nc, es):
        self.nc = nc
        self.es = es
        self.E = {"pe": nc.tensor, "act": nc.scalar, "dve": nc.vector, "pool": nc.gpsimd, "sp": nc.sync}
        self.sems = []
        self.cnt = []
        self.esem = {}
        for e in ["pe", "act", "dve", "pool"]:
            self.esem[e] = self.newsem("e_" + e)
        self.dpool = {q: [self.newsem(f"d_{q}{i}") for i in range(ND)] for q in ["sp", "pool"]}
        self.dnext = {q: 0 for q in self.dpool}
        self.waited = {e: {} for e in self.E}
        self.psi = 0

    def newsem(self, name):
        h = self.es.enter_context(self.nc.semaphore(name))
        self.sems.append(h)
        self.cnt.append(0)
        return len(self.sems) - 1

    def sb(self, name, shape, dt, es=None, nb=1):
        es = es or self.es
        self.nname = getattr(self, "nname", 0) + 1
        return T(es.enter_context(self.nc.sbuf_tensor(f"{name}_{self.nname}", list(shape), dt)), nb)

    def wait(self, e, toks):
        w = self.waited[e]
        need = {}
        for tk in toks:
            if tk is None:
                continue
            si, v = tk
            if e == "pe" and si == self.esem["pe"]:
                continue
            if w.get(si, 0) >= v:
                continue
            if need.get(si, 0) < v:
                need[si] = v
        for si, v in need.items():
            self.E[e].wait_ge(self.sems[si], v)
            w[si] = v

    def _deps(self, reads, writes):
        deps = []
        for b in reads:
            if b.w is not None:
                deps.append(b.w)
        for b in writes:
            if b.w is not None:
                deps.append(b.w)
            deps.extend(b.r.values())
        return deps

    def op(self, e, fn, reads=(), writes=()):
        reads = _bufs(reads)
        writes = _bufs(writes)
        self.wait(e, self._deps(reads, writes))
        ins = fn()
        si = self.esem[e]
        self.cnt[si] += 1
        ins.then_inc(self.sems[si], 1)
        tok = (si, self.cnt[si])
        for b in reads:
            b.r[e] = tok
        for b in writes:
            b.w = tok
            b.r = {}
        return tok

    def dma(self, q, out, in_, reads=(), writes=(), **kw):
        reads = _bufs(reads)
        writes = _bufs(writes)
        deps = self._deps(reads, writes)
        i = self.dnext[q]
        self.dnext[q] = (i + 1) % ND
        si = self.dpool[q][i]
        if self.cnt[si] > 0:
            deps.append((si, self.cnt[si]))
        self.wait(q, deps)
        ins = self.E[q].dma_start(out=out, in_=in_, **kw)
        self.cnt[si] += 16
        ins.then_inc(self.sems[si], 16)
        tok = (si, self.cnt[si])
        for b in reads:
            b.r[("d", si)] = tok
        for b in writes:
            b.w = tok
            b.r = {}
        return tok

    def barrier(self, engines=("pe", "act", "dve", "pool", "sp")):
        toks = [(si, self.cnt[si]) for si in range(len(self.sems)) if self.cnt[si] > 0]
        for e in engines:
            self.wait(e, toks)


def _gblock(kb, r):
    tt, half, ib = kb // 8, (kb % 8) // 4, kb % 4
    rho = r if half == 0 else 1 - r
    return 8 * tt + 2 * ib + rho


_OFF = [0, 8, 24, 48]


def host_consts(r):
    bf = ml_dtypes.bfloat16
    c = {}
    c["ident_f"] = np.eye(128, dtype=np.float32)
    c["ident_b"] = np.eye(128, dtype=np.float32).astype(bf)
    c["ones_f"] = np.ones((128, 128), np.float32)
    c["ones_b"] = np.ones((128, 128), np.float32).astype(bf)
    e0 = np.zeros((128, 1), np.float32)
    e0[0, 0] = 1.0
    c["e0"] = e0
    slopes = np.exp2(-8.0 * np.arange(1, 9, dtype=np.float64) / 8)
    n = np.arange(2048)
    rq = n % 128
    s = (n // 128) % 4
    qa = np.zeros((16, 2048), np.float32)
    for h in range(8):
        qa[2 * h] = -8.0 * slopes[h] * rq
        qa[2 * h + 1] = -8.0 * slopes[h] * 256.0 * s
    c["qaug"] = qa.astype(bf)
    ab = np.zeros((128, 640), np.float32)
    rk = np.arange(128)
    for h in range(8):
        for ci in range(4):
            for kb in range(8 * ci + 8):
                ab[:, h * 80 + _OFF[ci] + kb] = slopes[h] * (128.0 * _gblock(kb, r) + rk - 128.0 * (8 * ci + r))
    c["abias"] = ab
    mk = np.zeros((128, 8, 512), np.float32)
    for ti in range(8):
        half, ib = ti // 4, ti % 4
        rho = r if half == 0 else 1 - r
        keyp = 128 * (2 * ib + rho) + rk
        for sl in range(4):
            qp = 128 * (2 * sl + r) + np.arange(128)
            mk[:, ti, sl * 128:(sl + 1) * 128] = np.where(keyp[:, None] <= qp[None, :], 0.0, NEG)
    c["maskadd"] = mk.astype(bf)
    sel = np.zeros((128, 64, 128), np.float32)
    for gg in range(8):
        for j in range(8):
            for h in range(16):
                sel[gg * 16 + h, gg * 8 + j, j * 16 + h] = 1.0
    c["sel"] = sel.astype(bf)
    jj = np.arange(128) // 16
    c["maskT"] = (jj[None, :] >= jj[:, None]).astype(np.float32)
    l = np.arange(128)
    ib = (l % 64) // 16
    cc = l % 16
    rho = np.where(l < 64, r, 1 - r)
    pos = 16 * (2 * ib + rho) + cc
    c["tp"] = (pos[:, None] <= pos[None, :]).astype(np.float32)
    c["cs"] = np.broadcast_to((pos == 127)[:, None], (128, 128)).astype(np.float32).copy()
    pown = pos[:64]
    c["ss"] = (pos[:, None] == (pown[None, :] - 1)).astype(np.float32)
    c["ssprev"] = ((pos[:, None] == 127) & (pown[None, :] == 0)).astype(np.float32)
    pp = np.zeros((128, 4), np.float32)
    pp[:, 0] = 8.0 * (pos + 1)
    pp[:, 1] = -8.0 * (pos + 1)
    pp[:, 2] = (pos + 1)
    c["posp"] = pp
    return c


CONST_SPECS = [
    ("ident_f", (128, 128), F32), ("ident_b", (128, 128), BF16), ("ones_f", (128, 128), F32),
    ("ones_b", (128, 128), BF16), ("e0", (128, 1), F32), ("qaug", (16, 2048), BF16),
    ("abias", (128, 640), F32), ("maskadd", (128, 8, 512), BF16), ("sel", (128, 64, 128), BF16),
    ("maskT", (128, 128), F32), ("tp", (128, 128), F32), ("cs", (128, 128), F32),
    ("ss", (128, 64), F32), ("ssprev", (128, 64), F32), ("posp", (128, 4), F32),
]

IN_SPECS = [
    ("xl", (4096, 2048)),
    ("ffn1_norm", (2048,)), ("ffn1_w_gate", (2048, 5632)), ("ffn1_w_up", (2048, 5632)), ("ffn1_w_down", (5632, 2048)),
    ("mix_norm", (2048,)), ("w_in", (2048, 4096)),
    ("lambda_q1", (64,)), ("lambda_k1", (64,)), ("lambda_q2", (64,)), ("lambda_k2", (64,)), ("attn_subln", (128,)),
    ("ssm_lambda_re", (64, 64)), ("ssm_lambda_im", (64, 64)), ("ssm_log_dt", (64,)),
    ("ssm_b_re", (64, 64, 16)), ("ssm_b_im", (64, 64, 16)), ("ssm_c_re", (64, 16, 64)), ("ssm_c_im", (64, 16, 64)),
    ("ssm_d", (1024,)), ("ssm_glu_w", (1024, 1024)), ("ssm_glu_b", (1024,)), ("w_out", (2048, 2048)),
    ("ffn2_norm", (2048,)), ("ffn2_w_gate", (2048, 5632)), ("ffn2_w_up", (2048, 5632)), ("ffn2_w_down", (5632, 2048)),
    ("final_norm", (2048,)),
]


def build():
    nc = bass.Bass("TRN2", target_bir_lowering=False)
    I = {}
    for n, s in IN_SPECS:
        I[n] = nc.dram_tensor(n, list(s), F32, kind="ExternalInput").ap()
    Cd = {}
    for n, s, dt in CONST_SPECS:
        Cd[n] = nc.dram_tensor("c_" + n, list(s), dt, kind="ExternalInput").ap()
    out = nc.dram_tensor("out", [2048, 2048], F32, kind="ExternalOutput").ap()
    skind = "ExternalOutput" if DBG else "Internal"
    x1s = nc.dram_tensor("x1s", [16, 128, 2048], F32, kind=skind).ap()
    QT = nc.dram_tensor("QT", [16, 64, 2048], BF16, kind=skind).ap()
    KT = nc.dram_tensor("KT", [16, 64, 4096], BF16, kind=skind).ap()
    Vd = nc.dram_tensor("Vd", [4096, 1024], BF16, kind=skind).ap()
    uTd = nc.dram_tensor("uTd", [8, 128, 4096], BF16, kind=skind).ap()
    mixT = nc.dram_tensor("mixT", [16, 128, 2048], BF16, kind=skind).ap()

    with contextlib.ExitStack() as es:
        kb = KB(nc, es)
        ps = [T(es.enter_context(nc.psum_tensor(f"ps{i}", [128, 512], F32))) for i in range(8)]

        def nps():
            kb.psi = (kb.psi + 1) % 8
            return ps[kb.psi]

        cst = {}
        for n in ["ident_f", "ident_b", "ones_f", "ones_b", "e0"]:
            spec = [x for x in CONST_SPECS if x[0] == n][0]
            cst[n] = kb.sb("k_" + n, spec[1], spec[2])
            kb.dma("sp", cst[n].t[:], Cd[n][:], writes=[cst[n]])
        ident_f, ident_b, ones_f, ones_b, e0 = (cst[n] for n in ["ident_f", "ident_b", "ones_f", "ones_b", "e0"])
        gcols = kb.sb("gcols", [128, 3, 16], F32)
        with nc.allow_non_contiguous_dma(reason="tiny gain vectors"):
            for i, n in enumerate(["ffn1_norm", "mix_norm", "ffn2_norm"]):
                kb.dma("sp", gcols.t[:, i, :], I[n].rearrange("(t p) -> p t", p=128), writes=[gcols])

        def evac(eng, out_ap, in_ap, reads, writes):
            if eng == "act":
                return kb.op("act", lambda: nc.scalar.copy(out=out_ap, in_=in_ap), reads=reads, writes=writes)
            return kb.op("dve", lambda: nc.vector.tensor_copy(out=out_ap, in_=in_ap), reads=reads, writes=writes)

        def rmsnorm_fm(xT, hT, gi, Tn, sq, rstd, tmp):
            nh = Tn // 512
            pp = [nps() for _ in range(nh)]
            for dt in range(16):
                for hf in range(nh):
                    s_ = sq[(dt * nh + hf) % 2]
                    if hf % 2 == 0:
                        kb.op("act", lambda: nc.scalar.activation(out=s_.t[:], in_=xT.t[:, dt, hf * 512:(hf + 1) * 512],
                                                                  func=AF.Square), reads=[xT.bs[dt]], writes=[s_])
                    else:
                        kb.op("dve", lambda: nc.vector.tensor_tensor(out=s_.t[:], in0=xT.t[:, dt, hf * 512:(hf + 1) * 512],
                                                                     in1=xT.t[:, dt, hf * 512:(hf + 1) * 512], op=ALU.mult),
                              reads=[xT.bs[dt]], writes=[s_])
                    kb.op("pe", lambda: nc.tensor.matmul(pp[hf].t[:], lhsT=ones_b.t[:], rhs=s_.t[:],
                                                         start=(dt == 0), stop=(dt == 15)),
                          reads=[s_, ones_b], writes=[pp[hf]])
            for hf in range(nh):
                kb.op("act", lambda: nc.scalar.activation(out=tmp.t[:, hf * 512:(hf + 1) * 512], in_=pp[hf].t[:],
                                                          func=AF.Ln, scale=1.0 / D, bias=1e-6),
                      reads=[pp[hf]], writes=[tmp])
            kb.op("act", lambda: nc.scalar.activation(out=rstd.t[:, :Tn], in_=tmp.t[:, :Tn], func=AF.Exp, scale=-0.5),
                  reads=[tmp], writes=[rstd])
            if hT is None:
                return
            for dt in range(16):
                kb.op("dve", lambda: nc.vector.scalar_tensor_tensor(
                    out=hT.t[:, dt, :Tn], in0=xT.t[:, dt, :Tn], scalar=gcols.t[:, gi, dt:dt + 1], in1=rstd.t[:, :Tn],
                    op0=ALU.mult, op1=ALU.mult), reads=[xT.bs[dt], rstd, gcols], writes=[hT])

        def ffn(xT, hT, wg, wu, wd, ph):
            aT = kb.sb("aT", [128, 12, 1024], BF16, ph, nb=12)
            gsl = [kb.sb(f"gsl{i}", [128, 16, 256], BF16, ph) for i in range(2)]
            usl = [kb.sb(f"usl{i}", [128, 16, 256], BF16, ph) for i in range(2)]
            dsl = [kb.sb(f"dsl{i}", [128, 12, 256], BF16, ph) for i in range(2)]
            sg = [kb.sb(f"sg{i}", [128, 512], F32, ph) for i in range(2)]
            wgv = wg.rearrange("(k p) n -> p k n", p=128)
            wuv = wu.rearrange("(k p) n -> p k n", p=128)
            wdv = wd.rearrange("(f p) n -> p f n", p=128)
            si = 0
            di = 0
            gi_ = 0
            for (f0, nf) in [(0, 12), (12, 12), (24, 10), (34, 10)]:
                for fp in range(nf // 2):
                    fa = f0 + 2 * fp
                    g_ = gsl[si % 2]
                    u_ = usl[si % 2]
                    si += 1
                    kb.dma("pool", g_.t[:], wgv[:, :, fa * 128:(fa + 2) * 128], writes=[g_])
                    kb.dma("pool", u_.t[:], wuv[:, :, fa * 128:(fa + 2) * 128], writes=[u_])
                    for j in range(2):
                        for hf in range(2):
                            pg = nps()
                            pu = nps()

                            def mm(p_, w_):
                                for k in range(16):
                                    ins = nc.tensor.matmul(p_.t[:], lhsT=w_.t[:, k, j * 128:(j + 1) * 128],
                                                           rhs=hT.t[:, k, hf * 512:(hf + 1) * 512],
                                                           start=(k == 0), stop=(k == 15))
                                return ins
                            kb.op("pe", lambda: mm(pg, g_), reads=[g_, hT], writes=[pg])
                            kb.op("pe", lambda: mm(pu, u_), reads=[u_, hT], writes=[pu])
                            s_ = sg[gi_ % 2]
                            gi_ += 1
                            kb.op("act", lambda: nc.scalar.activation(out=s_.t[:], in_=pg.t[:], func=AF.Silu),
                                  reads=[pg], writes=[s_])
                            ai = fa - f0 + j
                            kb.op("dve", lambda: nc.vector.tensor_tensor(out=aT.t[:, ai, hf * 512:(hf + 1) * 512],
                                                                         in0=s_.t[:], in1=pu.t[:], op=ALU.mult),
                                  reads=[s_, pu], writes=[aT.bs[ai]])
                for dp in range(8):
                    d_ = dsl[di % 2]
                    di += 1
                    kb.dma("pool", d_.t[:, :nf, :], wdv[:, f0:f0 + nf, dp * 256:(dp + 1) * 256], writes=[d_])
                    for j in range(2):
                        dt = dp * 2 + j
                        for hf in range(2):
                            py = nps()

                            def mm2():
                                for fi in range(nf):
                                    ins = nc.tensor.matmul(py.t[:], lhsT=d_.t[:, fi, j * 128:(j + 1) * 128],
                                                           rhs=aT.t[:, fi, hf * 512:(hf + 1) * 512],
                                                           start=(fi == 0), stop=(fi == nf - 1))
                                return ins
                            kb.op("pe", mm2, reads=[d_] + aT.bs[:nf], writes=[py])
                            kb.op("dve", lambda: nc.vector.scalar_tensor_tensor(
                                out=xT.t[:, dt, hf * 512:(hf + 1) * 512], in0=py.t[:], scalar=0.5,
                                in1=xT.t[:, dt, hf * 512:(hf + 1) * 512], op0=ALU.mult, op1=ALU.add),
                                reads=[py, xT.bs[dt]], writes=[xT.bs[dt]])

        def phase_A():
            w_in_v = I["w_in"].rearrange("(k p) n -> p k n", p=128)
            with contextlib.ExitStack() as pa:
                xT = kb.sb("xT", [128, 16, 1024], F32, pa, nb=16)
                hT = kb.sb("hT", [128, 16, 1024], BF16, pa)
                rstd = kb.sb("rstd", [128, 1024], F32, pa)
                tmp = kb.sb("ntmp", [128, 1024], F32, pa)
                sq = [kb.sb(f"sq{i}", [128, 512], BF16, pa) for i in range(2)]
                for tt in range(4):
                    with contextlib.ExitStack() as p1:
                        xin = [kb.sb(f"xin{i}", [128, 2048], F32, p1) for i in range(2)]
                        ei = 0
                        for bl in range(8):
                            xi = xin[bl % 2]
                            r0 = tt * 1024 + bl * 128
                            kb.dma("sp", xi.t[:], I["xl"][r0:r0 + 128, :], writes=[xi])
                            for q4 in range(4):
                                p_ = nps()

                                def tr():
                                    for j in range(4):
                                        dt = q4 * 4 + j
                                        ins = nc.tensor.transpose(out=p_.t[:, j * 128:(j + 1) * 128],
                                                                  in_=xi.t[:, dt * 128:(dt + 1) * 128],
                                                                  identity=ident_f.t[:])
                                    return ins
                                kb.op("pe", tr, reads=[xi, ident_f], writes=[p_])
                                evac("act" if ei % 2 else "dve",
                                     xT.t[:, q4 * 4:(q4 + 1) * 4, bl * 128:(bl + 1) * 128],
                                     p_.t[:].rearrange("p (a b) -> p a b", a=4), [p_],
                                     [xT.bs[q4 * 4 + j] for j in range(4)])
                                ei += 1
                        rmsnorm_fm(xT, hT, 0, 1024, sq, rstd, tmp)
                        ffn(xT, hT, I["ffn1_w_gate"], I["ffn1_w_up"], I["ffn1_w_down"], p1)
                        kb.dma("sp", x1s.rearrange("t p n -> p t n")[:, :, tt * 512:(tt + 1) * 512], xT.t[:, :, 0:512],
                               reads=[xT])
                        rmsnorm_fm(xT, hT, 1, 1024, sq, rstd, tmp)
                        kb.barrier()
                    with contextlib.ExitStack() as p2:
                        wsl = [kb.sb(f"wsl{i}", [128, 16, 256], BF16, p2) for i in range(2)]
                        qst = [kb.sb(f"qst{i}", [128, 2, 1024], BF16, p2) for i in range(2)]
                        vst = [kb.sb(f"vst{i}", [128, 8, 256], BF16, p2) for i in range(2)]
                        ust = [kb.sb(f"ust{i}", [128, 2, 1024], BF16, p2) for i in range(2)]
                        wi = 0
                        ei = 0
                        for part, base, dst, ntok in (("q", 0, QT, 512), ("k", 1024, KT, 1024)):
                            for sl in range(4):
                                w_ = wsl[wi % 2]
                                st_ = qst[wi % 2]
                                wi += 1
                                kb.dma("pool", w_.t[:], w_in_v[:, :, base + sl * 256: base + (sl + 1) * 256], writes=[w_])
                                for jp in range(2):
                                    for hf in range(ntok // 512):
                                        p_ = nps()

                                        def mm():
                                            for k in range(16):
                                                ins = nc.tensor.matmul(p_.t[:], lhsT=w_.t[:, k, jp * 128:(jp + 1) * 128],
                                                                       rhs=hT.t[:, k, hf * 512:(hf + 1) * 512],
                                                                       start=(k == 0), stop=(k == 15))
                                            return ins
                                        kb.op("pe", mm, reads=[w_, hT], writes=[p_])
                                        evac("act" if ei % 2 else "dve", st_.t[:, jp, hf * 512:(hf + 1) * 512],
                                             p_.t[:], [p_], [st_])
                                        ei += 1
                                c0_, c1_ = (tt * 512, tt * 512 + 512) if part == "q" else (tt * 1024, tt * 1024 + 1024)
                                kb.dma("sp", dst[sl * 4:(sl + 1) * 4, :, c0_:c1_].rearrange("(jp e) p n -> (e p) jp n", e=2),
                                       st_.t[:, :, 0:ntok], reads=[st_])
                        for sl in range(4):
                            w_ = wsl[wi % 2]
                            st_ = vst[wi % 2]
                            wi += 1
                            kb.dma("pool", w_.t[:], w_in_v[:, :, 2048 + sl * 256: 2048 + (sl + 1) * 256], writes=[w_])
                            for bl in range(8):
                                p_ = nps()

                                def mm():
                                    for k in range(16):
                                        ins = nc.tensor.matmul(p_.t[:, 0:256], lhsT=hT.t[:, k, bl * 128:(bl + 1) * 128],
                                                               rhs=w_.t[:, k, :], start=(k == 0), stop=(k == 15))
                                    return ins
                                kb.op("pe", mm, reads=[w_, hT], writes=[p_])
                                evac("act" if ei % 2 else "dve", st_.t[:, bl, :], p_.t[:, 0:256], [p_], [st_])
                                ei += 1
                            kb.dma("sp", Vd[tt * 1024:(tt + 1) * 1024, sl * 256:(sl + 1) * 256].rearrange("(b p) n -> p b n", p=128),
                                   st_.t[:], reads=[st_])
                        for sl in range(4):
                            w_ = wsl[wi % 2]
                            st_ = ust[wi % 2]
                            wi += 1
                            kb.dma("pool", w_.t[:], w_in_v[:, :, 3072 + sl * 256: 3072 + (sl + 1) * 256], writes=[w_])
                            for j in range(2):
                                for hf in range(2):
                                    p_ = nps()

                                    def mm():
                                        for k in range(16):
                                            ins = nc.tensor.matmul(p_.t[:], lhsT=w_.t[:, k, j * 128:(j + 1) * 128],
                                                                   rhs=hT.t[:, k, hf * 512:(hf + 1) * 512],
                                                                   start=(k == 0), stop=(k == 15))
                                        return ins
                                    kb.op("pe", mm, reads=[w_, hT], writes=[p_])
                                    evac("act" if ei % 2 else "dve", st_.t[:, j, hf * 512:(hf + 1) * 512], p_.t[:], [p_], [st_])
                                    ei += 1
                            kb.dma("sp", uTd[sl * 2:(sl + 1) * 2, :, tt * 1024:(tt + 1) * 1024].rearrange("t p n -> p t n"),
                                   st_.t[:], reads=[st_])
                        kb.barrier()

        def phase_C():
            w_out_v = I["w_out"].rearrange("(k p) n -> p k n", p=128)
            with contextlib.ExitStack() as pc:
                xT = kb.sb("xT", [128, 16, 1024], F32, pc, nb=16)
                hT = kb.sb("hT", [128, 16, 1024], BF16, pc)
                rstd = kb.sb("rstd", [128, 1024], F32, pc)
                tmp = kb.sb("ntmp", [128, 1024], F32, pc)
                sq = [kb.sb(f"sq{i}", [128, 512], BF16, pc) for i in range(2)]
                gfin = kb.sb("gfin", [128, 2048], F32, pc)
                rcol = kb.sb("rcol", [128, 8], F32, pc)
                kb.dma("sp", gfin.t[:], I["final_norm"].partition_broadcast(128), writes=[gfin])
                for ot in range(2):
                    c0 = ot * 1024
                    kb.dma("sp", hT.t[:], mixT.rearrange("t p n -> p t n")[:, :, c0:c0 + 1024], writes=[hT])
                    for q4 in range(4):
                        kb.dma("sp", xT.t[:, q4 * 4:(q4 + 1) * 4, :],
                               x1s.rearrange("t p n -> p t n")[:, q4 * 4:(q4 + 1) * 4, c0:c0 + 1024],
                               writes=[xT.bs[q4 * 4 + j] for j in range(4)])
                    with contextlib.ExitStack() as p0:
                        wsl = [kb.sb(f"wsl{i}", [128, 16, 256], BF16, p0) for i in range(2)]
                        for sl in range(8):
                            w_ = wsl[sl % 2]
                            kb.dma("pool", w_.t[:], w_out_v[:, :, sl * 256:(sl + 1) * 256], writes=[w_])
                            for j in range(2):
                                dt = sl * 2 + j
                                for hf in range(2):
                                    p_ = nps()

                                    def mm():
                                        for k in range(16):
                                            ins = nc.tensor.matmul(p_.t[:], lhsT=w_.t[:, k, j * 128:(j + 1) * 128],
                                                                   rhs=hT.t[:, k, hf * 512:(hf + 1) * 512],
                                                                   start=(k == 0), stop=(k == 15))
                                        return ins
                                    kb.op("pe", mm, reads=[w_, hT], writes=[p_])
                                    kb.op("dve", lambda: nc.vector.tensor_tensor(
                                        out=xT.t[:, dt, hf * 512:(hf + 1) * 512], in0=p_.t[:],
                                        in1=xT.t[:, dt, hf * 512:(hf + 1) * 512], op=ALU.add),
                                        reads=[p_, xT.bs[dt]], writes=[xT.bs[dt]])
                        kb.barrier()
                    with contextlib.ExitStack() as p1:
                        rmsnorm_fm(xT, hT, 2, 1024, sq, rstd, tmp)
                        ffn(xT, hT, I["ffn2_w_gate"], I["ffn2_w_up"], I["ffn2_w_down"], p1)
                        kb.barrier()
                    with contextlib.ExitStack() as p2:
                        ostb = [kb.sb(f"ost{i}", [128, 2048], F32, p2) for i in range(2)]
                        rmsnorm_fm(xT, None, 0, 1024, sq, rstd, tmp)
                        p_ = nps()

                        def mmr():
                            for bl in range(8):
                                ins = nc.tensor.matmul(p_.t[:, bl:bl + 1], lhsT=rstd.t[:, bl * 128:(bl + 1) * 128],
                                                       rhs=e0.t[:, 0:1], start=True, stop=True)
                            return ins
                        kb.op("pe", mmr, reads=[rstd, e0], writes=[p_])
                        evac("dve", rcol.t[:], p_.t[:, 0:8], [p_], [rcol])
                        for bl in range(8):
                            ost = ostb[bl % 2]
                            for q4 in range(4):
                                p_ = nps()

                                def tr():
                                    for j in range(4):
                                        dt = q4 * 4 + j
                                        ins = nc.tensor.transpose(out=p_.t[:, j * 128:(j + 1) * 128],
                                                                  in_=xT.t[:, dt, bl * 128:(bl + 1) * 128],
                                                                  identity=ident_f.t[:])
                                    return ins
                                kb.op("pe", tr, reads=[xT, ident_f], writes=[p_])
                                kb.op("dve", lambda: nc.vector.scalar_tensor_tensor(
                                    out=ost.t[:, q4 * 512:(q4 + 1) * 512], in0=p_.t[:], scalar=rcol.t[:, bl:bl + 1],
                                    in1=gfin.t[:, q4 * 512:(q4 + 1) * 512], op0=ALU.mult, op1=ALU.mult),
                                    reads=[p_, rcol, gfin], writes=[ost])
                            kb.dma("sp", out[c0 + bl * 128:c0 + (bl + 1) * 128, :], ost.t[:], reads=[ost])
                        kb.barrier()

        def phase_T():
            with contextlib.ExitStack() as pt_:
                QTa = [[kb.sb(f"qta{s}{c}", [66, 2048], BF16, pt_, nb=2) for c in range(2)] for s in range(2)]
                KTa = [[kb.sb(f"kta{s}{c}", [66, 4096], BF16, pt_, nb=2) for c in range(2)] for s in range(2)]
                Vh = [kb.sb(f"vh{s}", [128, 32, 128], BF16, pt_) for s in range(2)]
                mask = kb.sb("mask", [128, 8, 512], BF16, pt_)
                abias = kb.sb("abias", [128, 640], F32, pt_)
                pT = [kb.sb(f"pT{i}", [128, 512], BF16, pt_) for i in range(6)]
                Lacc = [kb.sb(f"Lacc{i}", [128, 512], F32, pt_) for i in range(2)]
                f1 = kb.sb("f1", [128, 512], F32, pt_)
                f2 = kb.sb("f2", [128, 512], F32, pt_)
                f3 = kb.sb("f3", [128, 512], F32, pt_)
                f4 = kb.sb("f4", [128, 512], F32, pt_)
                f5 = kb.sb("f5", [128, 512], BF16, pt_)
                aost = [kb.sb(f"aost{i}", [128, 512], BF16, pt_) for i in range(2)]
                lamb = kb.sb("lamb", [128, 4, 64], F32, pt_)
                lsm = kb.sb("lsm", [128, 8], F32, pt_)
                kb.dma("sp", mask.t[:], Cd["maskadd"][:], writes=[mask])
                kb.dma("sp", abias.t[:], Cd["abias"][:], writes=[abias])
                for i, n in enumerate(["lambda_q1", "lambda_k1", "lambda_q2", "lambda_k2"]):
                    kb.dma("sp", lamb.t[:, i, :], I[n].partition_broadcast(128), writes=[lamb])
                with nc.allow_non_contiguous_dma(reason="tiny"):
                    kb.dma("sp", lsm.t[:, 7:8], I["attn_subln"].rearrange("(p o) -> p o", o=1), writes=[lsm])
                kb.op("dve", lambda: nc.vector.tensor_tensor(out=lamb.t[:, 0, :], in0=lamb.t[:, 0, :], in1=lamb.t[:, 1, :], op=ALU.mult),
                      reads=[lamb], writes=[lamb])
                kb.op("dve", lambda: nc.vector.tensor_tensor(out=lamb.t[:, 2, :], in0=lamb.t[:, 2, :], in1=lamb.t[:, 3, :], op=ALU.mult),
                      reads=[lamb], writes=[lamb])
                kb.op("dve", lambda: nc.vector.reduce_sum(out=lsm.t[:, 0:1], in_=lamb.t[:, 0, :], axis=AX.X), reads=[lamb], writes=[lsm])
                kb.op("dve", lambda: nc.vector.reduce_sum(out=lsm.t[:, 1:2], in_=lamb.t[:, 2, :], axis=AX.X), reads=[lamb], writes=[lsm])
                kb.op("act", lambda: nc.scalar.activation(out=lsm.t[:, 2:4], in_=lsm.t[:, 0:2], func=AF.Exp), reads=[lsm], writes=[lsm])
                kb.op("dve", lambda: nc.vector.scalar_tensor_tensor(out=lsm.t[:, 4:5], in0=lsm.t[:, 3:4], scalar=-0.2,
                                                                    in1=lsm.t[:, 2:3], op0=ALU.add, op1=ALU.subtract),
                      reads=[lsm], writes=[lsm])
                kb.op("dve", lambda: nc.vector.tensor_scalar(out=lsm.t[:, 5:6], in0=lsm.t[:, 7:8], scalar1=0.8, scalar2=None,
                                                             op0=ALU.mult), reads=[lsm], writes=[lsm])
                for s in range(2):
                    for c in range(2):
                        kb.op("dve", lambda: nc.vector.memset(KTa[s][c].t[64:66, :], 1.0), writes=[KTa[s][c].bs[1]])
                Vv = Vd.rearrange("(b p) d -> p b d", p=128)
                sidx = 0
                pidx = 0
                oi = 0
                for h in range(8):
                    st = h % 2
                    for c in range(2):
                        kb.dma("sp", QTa[st][c].t[0:64, :], QT[h * 2 + c], writes=[QTa[st][c].bs[0]])
                        kb.dma("sp", QTa[st][c].t[64:66, :], Cd["qaug"][2 * h:2 * h + 2, :], writes=[QTa[st][c].bs[1]])
                        kb.dma("sp", KTa[st][c].t[0:64, :], KT[h * 2 + c], writes=[KTa[st][c].bs[0]])
                    kb.dma("sp", Vh[st].t[:], Vv[:, :, h * 128:(h + 1) * 128], writes=[Vh[st]])
                    for ci in range(4):
                        nkb = 8 * ci + 8
                        steps = [(k_, c) for k_ in range(nkb) for c in range(2)]
                        info = {}

                        def lo_of(k_):
                            return ((k_ - 8 * ci) % 4) * 128 if k_ >= 8 * ci else 0

                        def qk(step):
                            nonlocal sidx, pidx
                            k_, c = step
                            tail = k_ >= 8 * ci
                            lo = lo_of(k_)
                            p_ = ps[4 + sidx % 4]
                            sidx += 1
                            pt = pT[pidx % 6]
                            pidx += 1

                            def f():
                                ins = nc.tensor.matmul(p_.t[:, lo:512], lhsT=KTa[st][c].t[0:66, k_ * 128:(k_ + 1) * 128],
                                                       rhs=QTa[st][c].t[0:66, ci * 512 + lo:(ci + 1) * 512],
                                                       start=True, stop=not tail)
                                if tail:
                                    ins = nc.tensor.matmul(p_.t[:, lo:512], lhsT=ident_b.t[:], rhs=mask.t[:, k_ - 8 * ci, lo:512],
                                                           start=False, stop=True)
                                return ins
                            kb.op("pe", f, reads=[KTa[st][c], QTa[st][c], mask, ident_b], writes=[p_])
                            col = h * 80 + _OFF[ci] + k_
                            kb.op("act", lambda: nc.scalar.activation(out=pt.t[:, lo:512], in_=p_.t[:, lo:512], func=AF.Exp,
                                                                      bias=abias.t[:, col:col + 1], scale=0.125),
                                  reads=[p_, abias], writes=[pt])
                            info[step] = pt

                        def pv(step):
                            k_, c = step
                            pt = info[step]
                            lo = lo_of(k_)
                            if k_ % 2 == 0:
                                def g():
                                    nc.tensor.matmul(ps[c].t[:, lo:512], lhsT=Vh[st].t[:, k_, :], rhs=pt.t[:, lo:512],
                                                     start=(k_ == 0), stop=(k_ == nkb - 1))
                                    return nc.tensor.matmul(ps[2 + c].t[:, lo:512], lhsT=ones_b.t[:], rhs=pt.t[:, lo:512],
                                                            start=(k_ == 0), stop=False)
                                kb.op("pe", g, reads=[Vh[st], pt, ones_b], writes=[ps[c], ps[2 + c]])
                            else:
                                kb.op("pe", lambda: nc.tensor.matmul(ps[c].t[:, lo:512], lhsT=Vh[st].t[:, k_, :], rhs=pt.t[:, lo:512],
                                                                     start=False, stop=(k_ == nkb - 1)),
                                      reads=[Vh[st], pt], writes=[ps[c]])
                                kb.op("dve", lambda: nc.vector.tensor_tensor(out=Lacc[c].t[:, lo:512], in0=Lacc[c].t[:, lo:512],
                                                                             in1=pt.t[:, lo:512], op=ALU.add),
                                      reads=[pt, Lacc[c]], writes=[Lacc[c]])
                        for c in range(2):
                            kb.op("pool", lambda: nc.gpsimd.memset(Lacc[c].t[:], 0.0), writes=[Lacc[c]])
                        LOOK = 3
                        for i in range(len(steps) + LOOK):
                            if i < len(steps):
                                qk(steps[i])
                            if i >= LOOK:
                                pv(steps[i - LOOK])
                        pl_ = []
                        for c in range(2):
                            kb.op("pe", lambda: nc.tensor.matmul(ps[2 + c].t[:], lhsT=ones_f.t[:], rhs=Lacc[c].t[:], start=False, stop=True),
                                  reads=[Lacc[c], ones_f], writes=[ps[2 + c]])
                            pl_.append(ps[2 + c])
                        kb.op("dve", lambda: nc.vector.tensor_copy(out=f2.t[:], in_=ps[0].t[:]), reads=[ps[0]], writes=[f2])
                        kb.op("act", lambda: nc.scalar.activation(out=f1.t[:], in_=pl_[0].t[:], func=AF.Ln), reads=[pl_[0]], writes=[f1])
                        kb.op("dve", lambda: nc.vector.tensor_copy(out=f3.t[:], in_=ps[1].t[:]), reads=[ps[1]], writes=[f3])
                        kb.op("act", lambda: nc.scalar.activation(out=f4.t[:], in_=pl_[1].t[:], func=AF.Ln), reads=[pl_[1]], writes=[f4])
                        kb.op("act", lambda: nc.scalar.activation(out=f1.t[:], in_=f1.t[:], func=AF.Exp, scale=-1.0), reads=[f1], writes=[f1])
                        kb.op("act", lambda: nc.scalar.activation(out=f4.t[:], in_=f4.t[:], func=AF.Exp, scale=-1.0), reads=[f4], writes=[f4])
                        kb.op("dve", lambda: nc.vector.tensor_tensor(out=f2.t[:], in0=f2.t[:], in1=f1.t[:], op=ALU.mult),
                              reads=[f2, f1], writes=[f2])
                        kb.op("dve", lambda: nc.vector.tensor_tensor(out=f3.t[:], in0=f3.t[:], in1=f4.t[:], op=ALU.mult),
                              reads=[f3, f4], writes=[f3])
                        kb.op("dve", lambda: nc.vector.scalar_tensor_tensor(out=f2.t[:], in0=f3.t[:], scalar=lsm.t[:, 4:5],
                                                                            in1=f2.t[:], op0=ALU.mult, op1=ALU.add),
                              reads=[f3, f2, lsm], writes=[f2])
                        kb.op("dve", lambda: nc.vector.tensor_tensor(out=f5.t[:], in0=f2.t[:], in1=f2.t[:], op=ALU.mult),
                              reads=[f2], writes=[f5])
                        p_ = ps[4 + sidx % 4]
                        sidx += 1
                        kb.op("pe", lambda: nc.tensor.matmul(p_.t[:], lhsT=ones_b.t[:], rhs=f5.t[:], start=True, stop=True),
                              reads=[f5, ones_b], writes=[p_])
                        kb.op("act", lambda: nc.scalar.activation(out=f4.t[:], in_=p_.t[:], func=AF.Ln, scale=1.0 / 128, bias=1e-5),
                              reads=[p_], writes=[f4])
                        kb.op("act", lambda: nc.scalar.activation(out=f1.t[:], in_=f4.t[:], func=AF.Exp, scale=-0.5),
                              reads=[f4], writes=[f1])
                        ao = aost[oi % 2]
                        oi += 1
                        kb.op("dve", lambda: nc.vector.scalar_tensor_tensor(out=ao.t[:], in0=f2.t[:], scalar=lsm.t[:, 5:6],
                                                                            in1=f1.t[:], op0=ALU.mult, op1=ALU.mult),
                              reads=[f2, f1, lsm], writes=[ao])
                        kb.dma("sp", mixT[h, :, ci * 512:(ci + 1) * 512], ao.t[:], reads=[ao])
                kb.barrier()

        def phase_S():
            TWO_PI = 2.0 * PI

            def rr_sin(x_ap, shift, out_ap, t_ap, ti_ap, r_ap, rd, wr):
                kb.op("dve", lambda: nc.vector.tensor_scalar_add(out=r_ap, in0=x_ap, scalar1=float(shift)), reads=rd, writes=wr)
                kb.op("dve", lambda: nc.vector.tensor_scalar_mul(out=t_ap, in0=r_ap, scalar1=1.0 / TWO_PI), reads=wr, writes=wr)
                kb.op("dve", lambda: nc.vector.tensor_copy(out=ti_ap, in_=t_ap), reads=wr, writes=wr)
                kb.op("dve", lambda: nc.vector.tensor_copy(out=t_ap, in_=ti_ap), reads=wr, writes=wr)
                kb.op("dve", lambda: nc.vector.scalar_tensor_tensor(out=r_ap, in0=t_ap, scalar=-TWO_PI, in1=r_ap,
                                                                    op0=ALU.mult, op1=ALU.add), reads=wr, writes=wr)
                kb.op("dve", lambda: nc.vector.tensor_scalar(out=r_ap, in0=r_ap, scalar1=-PI, scalar2=PI,
                                                             op0=ALU.max, op1=ALU.min), reads=wr, writes=wr)
                if out_ap is not None:
                    kb.op("act", lambda: nc.scalar.activation(out=out_ap, in_=r_ap, func=AF.Sin), reads=wr, writes=wr)

            def tt(out_ap, a_ap, b_ap, op, rd, wr):
                kb.op("dve", lambda: nc.vector.tensor_tensor(out=out_ap, in0=a_ap, in1=b_ap, op=op), reads=rd, writes=wr)

            with contextlib.ExitStack() as pS:
                sel = kb.sb("sel", [128, 64, 128], BF16, pS)
                kb.dma("sp", sel.t[:], Cd["sel"][:], writes=[sel])
                kc = {}
                for n in ["maskT", "tp", "cs", "ss", "ssprev", "posp"]:
                    spec = [x for x in CONST_SPECS if x[0] == n][0]
                    kc[n] = kb.sb("k_" + n, spec[1], spec[2], pS)
                    kb.dma("sp", kc[n].t[:], Cd[n][:], writes=[kc[n]])
                gT = kb.sb("gT", [128, 8, 2048], BF16, pS, nb=8)
                drep = kb.sb("drep", [128, 64], F32, pS)
                G = kb.sb("plG", [64, 26, 64], F32, pS)
                Gi = kb.sb("plGi", [64, 64], I32, pS)
                Pw = kb.sb("Pw", [64, 16, 2, 64], F32, pS)
                Pinv = kb.sb("Pinv", [64, 8, 2, 64], F32, pS)
                bb = kb.sb("bb", [64, 2, 64, 16], F32, pS)
                CT = kb.sb("CT", [64, 2, 1024], F32, pS)
                p0 = contextlib.ExitStack()
                braw = kb.sb("braw", [64, 2, 64, 16], F32, p0)
                cin = kb.sb("cin", [128, 2, 8, 64], F32, p0)
                bt1 = kb.sb("bt1", [64, 64, 16], F32, p0)
                bt2 = kb.sb("bt2", [64, 64, 16], F32, p0)
                PL = [G, Gi, Pw, Pinv]
                with nc.allow_non_contiguous_dma(reason="tiny parameter tables"):
                    for j in range(8):
                        kb.dma("sp", drep.t[j * 16:(j + 1) * 16, :], I["ssm_d"].rearrange("(g h) -> h g", h=16), writes=[drep])
                    kb.dma("sp", G.t[:, 0, :], I["ssm_lambda_re"].rearrange("g p -> p g"), writes=[G])
                    kb.dma("sp", G.t[:, 1, :], I["ssm_lambda_im"].rearrange("g p -> p g"), writes=[G])
                kb.dma("sp", G.t[:, 2, :], I["ssm_log_dt"].partition_broadcast(64), writes=[G])
                kb.dma("sp", braw.t[:, 0], I["ssm_b_re"].rearrange("g p h -> p g h"), writes=[braw])
                kb.dma("sp", braw.t[:, 1], I["ssm_b_im"].rearrange("g p h -> p g h"), writes=[braw])
                kb.dma("sp", cin.t[:, 0], I["ssm_c_re"].rearrange("g h p -> (g h) p").rearrange("(t q) p -> q t p", q=128), writes=[cin])
                kb.dma("sp", cin.t[:, 1], I["ssm_c_im"].rearrange("g h p -> (g h) p").rearrange("(t q) p -> q t p", q=128), writes=[cin])
                g = lambda i: G.t[:, i, :]
                LR, LI, DT, LDR, LDI, MAG, MAGI, SIN, COS, ARE, AIM, IRE, IIM, DEN, AM1, FRE, FIM, T1, T2, R1, R2 = range(21)
                kb.op("act", lambda: nc.scalar.activation(out=g(DT), in_=g(DT), func=AF.Exp), reads=PL, writes=PL)
                tt(g(LDR), g(LR), g(DT), ALU.mult, PL, PL)
                tt(g(LDI), g(LI), g(DT), ALU.mult, PL, PL)
                kb.op("act", lambda: nc.scalar.activation(out=g(MAG), in_=g(LDR), func=AF.Exp), reads=PL, writes=PL)
                kb.op("act", lambda: nc.scalar.activation(out=g(MAGI), in_=g(LDR), func=AF.Exp, scale=-1.0), reads=PL, writes=PL)
                rr_sin(g(LDI), 0.0, g(SIN), g(R1), Gi.t[:], g(R2), PL, PL)
                rr_sin(g(LDI), PI / 2, g(COS), g(R1), Gi.t[:], g(R2), PL, PL)
                tt(g(ARE), g(MAG), g(COS), ALU.mult, PL, PL)
                tt(g(AIM), g(MAG), g(SIN), ALU.mult, PL, PL)
                tt(g(IRE), g(MAGI), g(COS), ALU.mult, PL, PL)
                kb.op("dve", lambda: nc.vector.scalar_tensor_tensor(out=g(IIM), in0=g(MAGI), scalar=-1.0, in1=g(SIN),
                                                                    op0=ALU.mult, op1=ALU.mult), reads=PL, writes=PL)
                tt(g(T1), g(LR), g(LR), ALU.mult, PL, PL)
                tt(g(T2), g(LI), g(LI), ALU.mult, PL, PL)
                tt(g(DEN), g(T1), g(T2), ALU.add, PL, PL)
                kb.op("dve", lambda: nc.vector.reciprocal(out=g(DEN), in_=g(DEN)), reads=PL, writes=PL)
                kb.op("dve", lambda: nc.vector.tensor_scalar_add(out=g(AM1), in0=g(ARE), scalar1=-1.0), reads=PL, writes=PL)
                tt(g(T1), g(AM1), g(LR), ALU.mult, PL, PL)
                tt(g(T2), g(AIM), g(LI), ALU.mult, PL, PL)
                tt(g(T1), g(T1), g(T2), ALU.add, PL, PL)
                tt(g(FRE), g(T1), g(DEN), ALU.mult, PL, PL)
                tt(g(T1), g(AIM), g(LR), ALU.mult, PL, PL)
                tt(g(T2), g(AM1), g(LI), ALU.mult, PL, PL)
                tt(g(T1), g(T1), g(T2), ALU.subtract, PL, PL)
                tt(g(FIM), g(T1), g(DEN), ALU.mult, PL, PL)

                def cpow(P_, n, xr, xi):
                    kb.op("dve", lambda: nc.vector.memset(P_.t[:, 0, 0, :], 1.0), reads=PL, writes=PL)
                    kb.op("dve", lambda: nc.vector.memset(P_.t[:, 0, 1, :], 0.0), reads=PL, writes=PL)
                    for j in range(1, n):
                        pr, pi = P_.t[:, j - 1, 0, :], P_.t[:, j - 1, 1, :]
                        tt(g(T1), pr, xr, ALU.mult, PL, PL)
                        tt(g(T2), pi, xi, ALU.mult, PL, PL)
                        tt(P_.t[:, j, 0, :], g(T1), g(T2), ALU.subtract, PL, PL)
                        tt(g(T1), pr, xi, ALU.mult, PL, PL)
                        tt(g(T2), pi, xr, ALU.mult, PL, PL)
                        tt(P_.t[:, j, 1, :], g(T1), g(T2), ALU.add, PL, PL)
                cpow(Pw, 16, g(ARE), g(AIM))
                cpow(Pinv, 8, g(IRE), g(IIM))
                fb = lambda i: G.t[:, i, :].unsqueeze(2).to_broadcast([64, 64, 16])
                PB = PL + [bb, braw, bt1, bt2]
                tt(bt1.t[:], fb(FRE), braw.t[:, 0], ALU.mult, PB, PB)
                tt(bt2.t[:], fb(FIM), braw.t[:, 1], ALU.mult, PB, PB)
                tt(bb.t[:, 0], bt1.t[:], bt2.t[:], ALU.subtract, PB, PB)
                tt(bt1.t[:], fb(FRE), braw.t[:, 1], ALU.mult, PB, PB)
                tt(bt2.t[:], fb(FIM), braw.t[:, 0], ALU.mult, PB, PB)
                tt(bb.t[:, 1], bt1.t[:], bt2.t[:], ALU.add, PB, PB)
                for part in range(2):
                    for half in range(2):
                        p_ = nps()

                        def trc():
                            for t4 in range(4):
                                ins = nc.tensor.transpose(out=p_.t[0:64, t4 * 128:(t4 + 1) * 128],
                                                          in_=cin.t[:, part, half * 4 + t4, :], identity=ident_f.t[:])
                            return ins
                        kb.op("pe", trc, reads=[cin, ident_f], writes=[p_])
                        evac("dve", CT.t[:, part, half * 512:(half + 1) * 512], p_.t[0:64, :], [p_], [CT])

                kb.barrier()
                p0.close()
                PB = PL + [bb]
                pb = contextlib.ExitStack()
                BjL = [kb.sb(f"Bj{i}", [64, 2, 8, 8, 16], BF16, pb) for i in range(1)]
                CjL = [kb.sb(f"Cj{i}", [64, 2, 8, 8, 16], BF16, pb) for i in range(1)]
                c1 = kb.sb("c1", [64, 8, 8, 16], F32, pb)
                c2 = kb.sb("c2", [64, 8, 8, 16], F32, pb)
                TgL = [kb.sb(f"Tg{i}", [128, 8, 128], BF16, pb) for i in range(1)]
                BsTL = [kb.sb(f"BsT{i}", [128, 8, 128], BF16, pb) for i in range(1)]
                Tm = kb.sb("Tm", [128, 128], F32, pb)
                ut = kb.sb("ut", [128, 4096], BF16, pb)
                U8 = kb.sb("U8", [128, 8, 512], BF16, pb)
                Sp = kb.sb("Sp", [128, 4, 8, 128], F32, pb, nb=4)
                tb = kb.sb("tb", [128, 4, 8, 64], F32, pb)
                tw = kb.sb("tw", [128, 6, 512], F32, pb)
                twi = kb.sb("twi", [128, 512], I32, pb)
                ldtb = kb.sb("ldtb", [128, 8], F32, pb)
                Z = kb.sb("Z", [128, 8, 2, 64], F32, pb)
                Wsb = kb.sb("Wsb", [128, 8, 2, 64], F32, pb)
                z1 = kb.sb("z1", [128, 8, 64], F32, pb)
                z2 = kb.sb("z2", [128, 8, 64], F32, pb)
                xiTL = [kb.sb(f"xiT{i}", [64, 2, 256], BF16, pb) for i in range(2)]
                xb = kb.sb("xb", [128, 4, 8, 128], BF16, pb, nb=4)
                ssb = kb.sb("ssb", [128, 2, 64], BF16, pb)
                kb.op("dve", lambda: nc.vector.tensor_copy(out=ssb.t[:, 0, :], in_=kc["ss"].t[:]), reads=[kc["ss"]], writes=[ssb])
                kb.op("dve", lambda: nc.vector.tensor_copy(out=ssb.t[:, 1, :], in_=kc["ssprev"].t[:]), reads=[kc["ssprev"]], writes=[ssb])
                ysbL = [kb.sb(f"ysb{i}", [128, 256], F32, pb) for i in range(2)]
                wqL = [kb.sb(f"wq{i}", [128, 256], F32, pb) for i in range(2)]
                sgmL = [kb.sb(f"sgm{i}", [128, 256], F32, pb) for i in range(2)]
                GsL = [kb.sb(f"Gs{i}", [128, 8, 256], BF16, pb) for i in range(1)]
                utr = kb.sb("utr", [128, 8, 512], BF16, pb)
                for bt in range(8):
                    g0 = 8 * bt
                    if True:
                        Bj, Cj, Tg, BsT, Gs = BjL[0], CjL[0], TgL[0], BsTL[0], GsL[0]
                        kb.dma("sp", ut.t[:], uTd[bt], writes=[ut])
                        kb.dma("sp", tw.t[:, 0, :], I["ssm_lambda_re"][g0:g0 + 8, :].rearrange("g p -> (g p)").partition_broadcast(128), writes=[tw])
                        kb.dma("sp", tw.t[:, 1, :], I["ssm_lambda_im"][g0:g0 + 8, :].rearrange("g p -> (g p)").partition_broadcast(128), writes=[tw])
                        kb.dma("sp", ldtb.t[:], I["ssm_log_dt"][g0:g0 + 8].partition_broadcast(128), writes=[ldtb])
                        PBJ = PB + [Bj, Cj, c1, c2, CT]
                        bc = lambda ap: ap.unsqueeze(2).to_broadcast([64, 8, 16])
                        b4 = lambda ap: ap.rearrange("p j g -> p g j").unsqueeze(3).to_broadcast([64, 8, 8, 16])
                        v4 = lambda ap: ap.unsqueeze(2).to_broadcast([64, 8, 8, 16])
                        pr, pi = b4(Pinv.t[:, :, 0, g0:g0 + 8]), b4(Pinv.t[:, :, 1, g0:g0 + 8])
                        br, bi = v4(bb.t[:, 0, g0:g0 + 8, :]), v4(bb.t[:, 1, g0:g0 + 8, :])
                        tt(c1.t[:], pr, br, ALU.mult, PBJ, PBJ)
                        tt(c2.t[:], pi, bi, ALU.mult, PBJ, PBJ)
                        tt(Bj.t[:, 0], c1.t[:], c2.t[:], ALU.subtract, PBJ, PBJ)
                        tt(c1.t[:], pr, bi, ALU.mult, PBJ, PBJ)
                        tt(c2.t[:], pi, br, ALU.mult, PBJ, PBJ)
                        tt(Bj.t[:, 1], c1.t[:], c2.t[:], ALU.add, PBJ, PBJ)

                        def build_C(joff):
                            cr = v4(CT.t[:, 0, g0 * 16:(g0 + 8) * 16].rearrange("p (g h) -> p g h", h=16))
                            ci_ = v4(CT.t[:, 1, g0 * 16:(g0 + 8) * 16].rearrange("p (g h) -> p g h", h=16))
                            pr_, pi_ = b4(Pw.t[:, joff:joff + 8, 0, g0:g0 + 8]), b4(Pw.t[:, joff:joff + 8, 1, g0:g0 + 8])
                            tt(c1.t[:], pr_, cr, ALU.mult, PBJ, PBJ)
                            tt(c2.t[:], pi_, ci_, ALU.mult, PBJ, PBJ)
                            tt(Cj.t[:, 0], c1.t[:], c2.t[:], ALU.subtract, PBJ, PBJ)
                            tt(c1.t[:], pr_, ci_, ALU.mult, PBJ, PBJ)
                            tt(c2.t[:], pi_, cr, ALU.mult, PBJ, PBJ)
                            kb.op("dve", lambda: nc.vector.scalar_tensor_tensor(
                                out=Cj.t[:, 1], in0=c1.t[:], scalar=-1.0, in1=c2.t[:],
                                op0=ALU.mult, op1=ALU.subtract), reads=PBJ, writes=PBJ)
                        build_C(0)
                        fl = lambda ap: ap.rearrange("p a b -> p (a b)")
                        for gl in range(8):
                            p_ = nps()

                            def mT():
                                nc.tensor.matmul(p_.t[:, 0:128], lhsT=fl(Bj.t[:, 0, gl]), rhs=fl(Cj.t[:, 0, gl]), start=True, stop=False)
                                return nc.tensor.matmul(p_.t[:, 0:128], lhsT=fl(Bj.t[:, 1, gl]), rhs=fl(Cj.t[:, 1, gl]), start=False, stop=True)
                            kb.op("pe", mT, reads=[Bj, Cj], writes=[p_])
                            kb.op("dve", lambda: nc.vector.tensor_tensor(out=Tm.t[:], in0=p_.t[:, 0:128], in1=kc["maskT"].t[:], op=ALU.mult),
                                  reads=[p_, kc["maskT"]], writes=[Tm])
                            kb.op("dve", lambda: nc.vector.scalar_tensor_tensor(
                                out=Tg.t[:, gl, :], in0=ident_f.t[:], scalar=drep.t[:, g0 + gl:g0 + gl + 1], in1=Tm.t[:],
                                op0=ALU.mult, op1=ALU.add), reads=[Tm, ident_f, drep], writes=[Tg])
                            p2 = nps()

                            def mB():
                                nc.tensor.matmul(p2.t[:, 0:64], lhsT=fl(Bj.t[:, 0, gl]), rhs=ident_b.t[0:64, 0:64], start=True, stop=True)
                                return nc.tensor.matmul(p2.t[:, 64:128], lhsT=fl(Bj.t[:, 1, gl]), rhs=ident_b.t[0:64, 0:64], start=True, stop=True)
                            kb.op("pe", mB, reads=[Bj, ident_b], writes=[p2])
                            evac("act", BsT.t[:, gl, :], p2.t[:, 0:128], [p2], [BsT])
                        build_C(8)
                        kb.op("act", lambda: nc.scalar.copy(out=utr.t[:], in_=ut.t[:].rearrange("p (c j) -> p j c", j=8)),
                              reads=[ut], writes=[utr])
                        for gl in range(8):
                            p_ = nps()

                            def mU():
                                for j in range(8):
                                    ins = nc.tensor.matmul(p_.t[:], lhsT=sel.t[:, gl * 8 + j, :], rhs=utr.t[:, j, :],
                                                           start=(j == 0), stop=(j == 7))
                                return ins
                            kb.op("pe", mU, reads=[sel, utr], writes=[p_])
                            evac("act", U8.t[:, gl, :], p_.t[:], [p_], [U8])
                            p3 = nps()

                            def mS():
                                for sb_ in range(4):
                                    ins = nc.tensor.matmul(p3.t[:, sb_ * 128:(sb_ + 1) * 128],
                                                           lhsT=U8.t[:, gl, sb_ * 128:(sb_ + 1) * 128], rhs=BsT.t[:, gl, :],
                                                           start=True, stop=True)
                                return ins
                            kb.op("pe", mS, reads=[U8, BsT], writes=[p3])
                            evac("act", Sp.t[:, :, gl, :], p3.t[:].rearrange("p (s n) -> p s n", s=4), [p3], [Sp])
                        TB = [tb, tw, twi, ldtb, kc["posp"]]
                        w_ = lambda i: tw.t[:, i, :]
                        w3 = lambda i: tw.t[:, i, :].rearrange("p (g q) -> p g q", q=64)
                        kb.op("act", lambda: nc.scalar.activation(out=ldtb.t[:], in_=ldtb.t[:], func=AF.Exp), reads=TB, writes=TB)
                        dtb = ldtb.t[:].unsqueeze(2).to_broadcast([128, 8, 64])
                        tt(w3(0), w3(0), dtb, ALU.mult, TB, TB)
                        tt(w3(1), w3(1), dtb, ALU.mult, TB, TB)
                        kb.op("act", lambda: nc.scalar.activation(out=w_(2), in_=w_(0), func=AF.Exp, scale=kc["posp"].t[:, 0:1]), reads=TB, writes=TB)
                        kb.op("act", lambda: nc.scalar.activation(out=w_(3), in_=w_(0), func=AF.Exp, scale=kc["posp"].t[:, 1:2]), reads=TB, writes=TB)
                        kb.op("dve", lambda: nc.vector.tensor_scalar_mul(out=w_(1), in0=w_(1), scalar1=8.0), reads=TB, writes=TB)
                        rr_sin(w_(1), 0.0, None, w_(4), twi.t[:], w_(5), TB, TB)
                        kb.op("dve", lambda: nc.vector.tensor_scalar_mul(out=w_(1), in0=w_(5), scalar1=kc["posp"].t[:, 2:3]), reads=TB, writes=TB)
                        rr_sin(w_(1), 0.0, w_(0), w_(4), twi.t[:], w_(5), TB, TB)
                        rr_sin(w_(1), PI / 2, w_(1), w_(4), twi.t[:], w_(5), TB, TB)
                        tbf = lambda i: tb.t[:, i].rearrange("p g q -> p (g q)")
                        tt(tbf(2), w_(2), w_(1), ALU.mult, TB, TB)
                        tt(tbf(3), w_(2), w_(0), ALU.mult, TB, TB)
                        tt(tbf(0), w_(3), w_(1), ALU.mult, TB, TB)
                        kb.op("dve", lambda: nc.vector.scalar_tensor_tensor(out=tbf(1), in0=w_(3), scalar=-1.0, in1=w_(0),
                                                                            op0=ALU.mult, op1=ALU.mult), reads=TB, writes=TB)
                        for sb_ in range(4):
                            sre, sim = Sp.t[:, sb_, :, 0:64], Sp.t[:, sb_, :, 64:128]
                            RZ = [Sp.bs[sb_], tb, z1, z2, Z]
                            tt(z1.t[:], sre, tb.t[:, 0], ALU.mult, RZ, [z1])
                            tt(z2.t[:], sim, tb.t[:, 1], ALU.mult, RZ, [z2])
                            tt(Z.t[:, :, 0, :], z1.t[:], z2.t[:], ALU.subtract, RZ, [Z])
                            tt(z1.t[:], sre, tb.t[:, 1], ALU.mult, RZ, [z1])
                            tt(z2.t[:], sim, tb.t[:, 0], ALU.mult, RZ, [z2])
                            tt(Z.t[:, :, 1, :], z1.t[:], z2.t[:], ALU.add, RZ, [Z])
                            for q in range(2):
                                p_ = nps()

                                def mW():
                                    ins = nc.tensor.matmul(p_.t[:], lhsT=kc["tp"].t[:],
                                                           rhs=Z.t[:, 4 * q:4 * q + 4].rearrange("p a b c -> p (a b c)"),
                                                           start=True, stop=(sb_ == 0))
                                    if sb_ > 0:
                                        ins = nc.tensor.matmul(p_.t[:], lhsT=kc["cs"].t[:],
                                                               rhs=Sp.t[:, sb_ - 1, 4 * q:4 * q + 4, :].rearrange("p a b -> p (a b)"),
                                                               start=False, stop=True)
                                    return ins
                                kb.op("pe", mW, reads=[Z, kc["tp"], kc["cs"]] + ([Sp.bs[sb_ - 1]] if sb_ > 0 else []), writes=[p_])
                                evac("act", Wsb.t[:, 4 * q:4 * q + 4].rearrange("p a b c -> p (a b c)"), p_.t[:], [p_], [Wsb])
                            wre, wim = Wsb.t[:, :, 0, :], Wsb.t[:, :, 1, :]
                            RX = [Wsb, tb, z1, z2, Sp.bs[sb_]]
                            tt(z1.t[:], wre, tb.t[:, 2], ALU.mult, RX, [z1])
                            tt(z2.t[:], wim, tb.t[:, 3], ALU.mult, RX, [z2])
                            tt(sre, z1.t[:], z2.t[:], ALU.subtract, RX, [Sp.bs[sb_]])
                            tt(z1.t[:], wre, tb.t[:, 3], ALU.mult, RX, [z1])
                            tt(z2.t[:], wim, tb.t[:, 2], ALU.mult, RX, [z2])
                            tt(sim, z1.t[:], z2.t[:], ALU.add, RX, [Sp.bs[sb_]])
                            evac("act", xb.t[:, sb_], Sp.t[:, sb_], [Sp.bs[sb_]], [xb.bs[sb_]])
                        for gl in range(8):
                            xiT, ysb, wq, sgm = xiTL[gl % 2], ysbL[gl % 2], wqL[gl % 2], sgmL[gl % 2]
                            p4 = nps()

                            def mX():
                                for part in range(2):
                                    for sb_ in range(4):
                                        o_ = p4.t[0:64, part * 256 + sb_ * 64: part * 256 + (sb_ + 1) * 64]
                                        ins = nc.tensor.matmul(o_, lhsT=xb.t[:, sb_, gl, part * 64:(part + 1) * 64],
                                                               rhs=ssb.t[:, 0, :], start=True, stop=(sb_ == 0))
                                        if sb_ > 0:
                                            ins = nc.tensor.matmul(o_, lhsT=xb.t[:, sb_ - 1, gl, part * 64:(part + 1) * 64],
                                                                   rhs=ssb.t[:, 1, :], start=False, stop=True)
                                return ins
                            kb.op("pe", mX, reads=[xb, ssb], writes=[p4])
                            evac("act", xiT.t[:], p4.t[0:64, :].rearrange("p (a b) -> p a b", a=2), [p4], [xiT])
                            p5 = nps()

                            def mY():
                                nc.tensor.matmul(p5.t[:, 0:256], lhsT=fl(Cj.t[:, 0, gl]), rhs=xiT.t[:, 0, :], start=True, stop=False)
                                nc.tensor.matmul(p5.t[:, 0:256], lhsT=fl(Cj.t[:, 1, gl]), rhs=xiT.t[:, 1, :], start=False, stop=False)
                                for t_ in range(4):
                                    ins = nc.tensor.matmul(p5.t[:, t_ * 64:(t_ + 1) * 64], lhsT=Tg.t[:, gl, :],
                                                           rhs=U8.t[:, gl, t_ * 128:t_ * 128 + 64], start=False, stop=(t_ == 3))
                                return ins
                            kb.op("pe", mY, reads=[Cj, xiT, Tg, U8], writes=[p5])
                            kb.op("act", lambda: nc.scalar.copy(out=ysb.t[:], in_=p5.t[:, 0:256]), reads=[p5], writes=[ysb])
                            kb.op("act", lambda: nc.scalar.activation(out=wq.t[:], in_=p5.t[:, 0:256], func=AF.Square), reads=[p5], writes=[wq])
                            kb.op("dve", lambda: nc.vector.tensor_scalar(out=wq.t[:], in0=wq.t[:], scalar1=0.044715, scalar2=1.0,
                                                                         op0=ALU.mult, op1=ALU.add), reads=[wq], writes=[wq])
                            tt(wq.t[:], wq.t[:], ysb.t[:], ALU.mult, [wq, ysb], [wq])
                            kb.op("act", lambda: nc.scalar.activation(out=sgm.t[:], in_=wq.t[:], func=AF.Sigmoid, scale=1.5957691216),
                                  reads=[wq], writes=[sgm])
                            tt(Gs.t[:, gl, :], ysb.t[:], sgm.t[:], ALU.mult, [ysb, sgm], [Gs])
                        gtv = gT.t[:, bt, :].rearrange("p (c j) -> p c j", j=8)
                        for j in range(8):
                            p_ = nps()

                            def mG():
                                for gg in range(8):
                                    ins = nc.tensor.matmul(p_.t[:, 0:256], lhsT=sel.t[:, j * 8 + gg, :], rhs=Gs.t[:, gg, :],
                                                           start=(gg == 0), stop=(gg == 7))
                                return ins
                            kb.op("pe", mG, reads=[sel, Gs], writes=[p_])
                            evac("act", gtv[:, :, j], p_.t[:, 0:256], [p_], [gT.bs[bt]])
                kb.barrier()
                pb.close()
                with contextlib.ExitStack() as pg:
                    gw = [kb.sb(f"gw{i}", [128, 8, 256], BF16, pg) for i in range(2)]
                    bcol = kb.sb("bcol", [128, 8], F32, pg)
                    sgl = [kb.sb(f"sgl{i}", [128, 512], F32, pg) for i in range(2)]
                    gost = [kb.sb(f"gost{i}", [128, 512], BF16, pg) for i in range(2)]
                    with nc.allow_non_contiguous_dma(reason="tiny"):
                        kb.dma("sp", bcol.t[:], I["ssm_glu_b"].rearrange("(t p) -> p t", p=128), writes=[bcol])
                    gv = I["ssm_glu_w"].rearrange("(k p) n -> p k n", p=128)
                    oi = 0
                    for sl in range(4):
                        w2 = gw[sl % 2]
                        kb.dma("pool", w2.t[:], gv[:, :, sl * 256:(sl + 1) * 256], writes=[w2])
                        for j in range(2):
                            m = sl * 2 + j
                            for hf in range(4):
                                p_ = nps()

                                def mm():
                                    for k in range(8):
                                        ins = nc.tensor.matmul(p_.t[:], lhsT=w2.t[:, k, j * 128:(j + 1) * 128],
                                                               rhs=gT.t[:, k, hf * 512:(hf + 1) * 512], start=(k == 0), stop=(k == 7))
                                    return ins
                                kb.op("pe", mm, reads=[w2, gT], writes=[p_])
                                s_ = sgl[oi % 2]
                                o_ = gost[oi % 2]
                                oi += 1
                                kb.op("act", lambda: nc.scalar.activation(out=s_.t[:], in_=p_.t[:], func=AF.Sigmoid, bias=bcol.t[:, m:m + 1]),
                                      reads=[p_, bcol], writes=[s_])
                                tt(o_.t[:], gT.t[:, m, hf * 512:(hf + 1) * 512], s_.t[:], ALU.mult, [gT.bs[m], s_], [o_])
                                kb.dma("sp", mixT[8 + m, :, hf * 512:(hf + 1) * 512], o_.t[:], reads=[o_])
                    kb.barrier()

        kb.barrier()
        if "A" in PHASES:
            phase_A()
        if "S" in PHASES:
            phase_S()
        if "T" in PHASES:
            phase_T()
        if "C" in PHASES:
            phase_C()
        kb.barrier()
    return nc


_NC = None


def _get_nc():
    global _NC
    if _NC is None:
        _NC = build()
    return _NC


def make_in_maps(inputs):
    x = np.asarray(inputs["x"], dtype=np.float32)
    shared = {}
    for n, s in IN_SPECS:
        if n == "xl":
            continue
        a = np.asarray(inputs[n], dtype=np.float32)
        shared[n] = np.ascontiguousarray(a.reshape(s))
    consts = [host_consts(r) for r in range(2)]
    in_maps = []
    for c in range(8):
        b, r = c // 2, c % 2
        xb = x[b].reshape(4, 4, 2, 128, D)
        own = xb[:, :, r]
        par = xb[:, :, 1 - r]
        xl = np.concatenate([own.reshape(4, 512, D), par.reshape(4, 512, D)], axis=1).reshape(4096, D)
        m = dict(shared)
        m["xl"] = np.ascontiguousarray(xl)
        for n, s, dt in CONST_SPECS:
            m["c_" + n] = consts[r][n]
        in_maps.append(m)
    return in_maps


def kernel(**inputs):
    nc = _get_nc()
    in_maps = make_in_maps(inputs)
    res = run_bass_kernel_spmd(nc, in_maps, core_ids=list(range(8)))
    y = np.zeros((4, 4096, D), np.float32)
    yv = y.reshape(4, 4, 4, 2, 128, D)
    for c in range(8):
        b, r = c // 2, c % 2
        o = np.asarray(res.results[c]["out"], dtype=np.float32).reshape(4, 4, 128, D)
        yv[b, :, :, r] = o
    if DBG:
        kernel.last = res
    return y
```

```python
import os
import math
import contextlib
import numpy as np
import ml_dtypes
import concourse.bass as bass
import concourse.mybir as mybir
from concourse.bass_utils import run_bass_kernel_spmd

F32 = mybir.dt.float32
BF16 = mybir.dt.bfloat16
I32 = mybir.dt.int32
AF = mybir.ActivationFunctionType
ALU = mybir.AluOpType
AX = mybir.AxisListType
ND = 12
PI = math.pi
D = 2048
DFF = 5632
NEG = -30000.0
DBG = os.environ.get("KDBG", "")
PHASES = os.environ.get("KPH", "ASTC")


class Buf:
    __slots__ = ("w", "r")

    def __init__(self):
        self.w = None
        self.r = {}


class T:
    def __init__(self, t, nb=1):
        self.t = t
        self.b = Buf()
        self.bs = [Buf() for _ in range(nb)] if nb > 1 else [self.b]


def _bufs(lst):
    out = []
    for x in lst:
        if isinstance(x, T):
            out.extend(x.bs)
        elif isinstance(x, (list, tuple)):
            out.extend(_bufs(x))
        else:
            out.append(x)
    return out


class KB:
    def __init__(self, nc, es):
        self.nc = nc
        self.es = es
        self.E = {"pe": nc.tensor, "act": nc.scalar, "dve": nc.vector, "pool": nc.gpsimd, "sp": nc.sync}
        self.sems = []
        self.cnt = []
        self.esem = {}
        for e in ["pe", "act", "dve", "pool"]:
            self.esem[e] = self.newsem("e_" + e)
        self.dpool = {q: [self.newsem(f"d_{q}{i}") for i in range(ND)] for q in ["sp", "pool"]}
        self.dnext = {q: 0 for q in self.dpool}
        self.waited = {e: {} for e in self.E}
        self.psi = 0

    def newsem(self, name):
        h = self.es.enter_context(self.nc.semaphore(name))
        self.sems.append(h)
        self.cnt.append(0)
        return len(self.sems) - 1

    def sb(self, name, shape, dt, es=None, nb=1):
        es = es or self.es
        self.nname = getattr(self, "nname", 0) + 1
        return T(es.enter_context(self.nc.sbuf_tensor(f"{name}_{self.nname}", list(shape), dt)), nb)

    def wait(self, e, toks):
        w = self.waited[e]
        need = {}
        for tk in toks:
            if tk is None:
                continue
            si, v = tk
            if e == "pe" and si == self.esem["pe"]:
                continue
            if w.get(si, 0) >= v:
                continue
            if need.get(si, 0) < v:
                need[si] = v
        for si, v in need.items():
            self.E[e].wait_ge(self.sems[si], v)
            w[si] = v

    def _deps(self, reads, writes):
        deps = []
        for b in reads:
            if b.w is not None:
                deps.append(b.w)
        for b in writes:
            if b.w is not None:
                deps.append(b.w)
            deps.extend(b.r.values())
        return deps

    def op(self, e, fn, reads=(), writes=()):
        reads = _bufs(reads)
        writes = _bufs(writes)
        self.wait(e, self._deps(reads, writes))
        ins = fn()
        si = self.esem[e]
        self.cnt[si] += 1
        ins.then_inc(self.sems[si], 1)
        tok = (si, self.cnt[si])
        for b in reads:
            b.r[e] = tok
        for b in writes:
            b.w = tok
            b.r = {}
        return tok

    def dma(self, q, out, in_, reads=(), writes=(), **kw):
        reads = _bufs(reads)
        writes = _bufs(writes)
        deps = self._deps(reads, writes)
        i = self.dnext[q]
        self.dnext[q] = (i + 1) % ND
        si = self.dpool[q][i]
        if self.cnt[si] > 0:
            deps.append((si, self.cnt[si]))
        self.wait(q, deps)
        ins = self.E[q].dma_start(out=out, in_=in_, **kw)
        self.cnt[si] += 16
        ins.then_inc(self.sems[si], 16)
        tok = (si, self.cnt[si])
        for b in reads:
            b.r[("d", si)] = tok
        for b in writes:
            b.w = tok
            b.r = {}
        return tok

    def barrier(self, engines=("pe", "act", "dve", "pool", "sp")):
        toks = [(si, self.cnt[si]) for si in range(len(self.sems)) if self.cnt[si] > 0]
        for e in engines:
            self.wait(e, toks)


def _gblock(kb, r):
    tt, half, ib = kb // 8, (kb % 8) // 4, kb % 4
    rho = r if half == 0 else 1 - r
    return 8 * tt + 2 * ib + rho


_OFF = [0, 8, 24, 48]


def host_consts(r):
    bf = ml_dtypes.bfloat16
    c = {}
    c["ident_f"] = np.eye(128, dtype=np.float32)
    c["ident_b"] = np.eye(128, dtype=np.float32).astype(bf)
    c["ones_f"] = np.ones((128, 128), np.float32)
    c["ones_b"] = np.ones((128, 128), np.float32).astype(bf)
    e0 = np.zeros((128, 1), np.float32)
    e0[0, 0] = 1.0
    c["e0"] = e0
    slopes = np.exp2(-8.0 * np.arange(1, 9, dtype=np.float64) / 8)
    n = np.arange(2048)
    rq = n % 128
    s = (n // 128) % 4
    qa = np.zeros((16, 2048), np.float32)
    for h in range(8):
        qa[2 * h] = -8.0 * slopes[h] * rq
        qa[2 * h + 1] = -8.0 * slopes[h] * 256.0 * s
    c["qaug"] = qa.astype(bf)
    ab = np.zeros((128, 640), np.float32)
    rk = np.arange(128)
    for h in range(8):
        for ci in range(4):
            for kb in range(8 * ci + 8):
                ab[:, h * 80 + _OFF[ci] + kb] = slopes[h] * (128.0 * _gblock(kb, r) + rk - 128.0 * (8 * ci + r))
    c["abias"] = ab
    mk = np.zeros((128, 8, 512), np.float32)
    for ti in range(8):
        half, ib = ti // 4, ti % 4
        rho = r if half == 0 else 1 - r
        keyp = 128 * (2 * ib + rho) + rk
        for sl in range(4):
            qp = 128 * (2 * sl + r) + np.arange(128)
            mk[:, ti, sl * 128:(sl + 1) * 128] = np.where(keyp[:, None] <= qp[None, :], 0.0, NEG)
    c["maskadd"] = mk.astype(bf)
    sel = np.zeros((128, 64, 128), np.float32)
    for gg in range(8):
        for j in range(8):
            for h in range(16):
                sel[gg * 16 + h, gg * 8 + j, j * 16 + h] = 1.0
    c["sel"] = sel.astype(bf)
    jj = np.arange(128) // 16
    c["maskT"] = (jj[None, :] >= jj[:, None]).astype(np.float32)
    l = np.arange(128)
    ib = (l % 64) // 16
    cc = l % 16
    rho = np.where(l < 64, r, 1 - r)
    pos = 16 * (2 * ib + rho) + cc
    c["tp"] = (pos[:, None] <= pos[None, :]).astype(np.float32)
    c["cs"] = np.broadcast_to((pos == 127)[:, None], (128, 128)).astype(np.float32).copy()
    pown = pos[:64]
    c["ss"] = (pos[:, None] == (pown[None, :] - 1)).astype(np.float32)
    c["ssprev"] = ((pos[:, None] == 127) & (pown[None, :] == 0)).astype(np.float32)
    pp = np.zeros((128, 4), np.float32)
    pp[:, 0] = 8.0 * (pos + 1)
    pp[:, 1] = -8.0 * (pos + 1)
    pp[:, 2] = (pos + 1)
    c["posp"] = pp
    return c


CONST_SPECS = [
    ("ident_f", (128, 128), F32), ("ident_b", (128, 128), BF16), ("ones_f", (128, 128), F32),
    ("ones_b", (128, 128), BF16), ("e0", (128, 1), F32), ("qaug", (16, 2048), BF16),
    ("abias", (128, 640), F32), ("maskadd", (128, 8, 512), BF16), ("sel", (128, 64, 128), BF16),
    ("maskT", (128, 128), F32), ("tp", (128, 128), F32), ("cs", (128, 128), F32),
    ("ss", (128, 64), F32), ("ssprev", (128, 64), F32), ("posp", (128, 4), F32),
]

IN_SPECS = [
    ("xl", (4096, 2048)),
    ("ffn1_norm", (2048,)), ("ffn1_w_gate", (2048, 5632)), ("ffn1_w_up", (2048, 5632)), ("ffn1_w_down", (5632, 2048)),
    ("mix_norm", (2048,)), ("w_in", (2048, 4096)),
    ("lambda_q1", (64,)), ("lambda_k1", (64,)), ("lambda_q2", (64,)), ("lambda_k2", (64,)), ("attn_subln", (128,)),
    ("ssm_lambda_re", (64, 64)), ("ssm_lambda_im", (64, 64)), ("ssm_log_dt", (64,)),
    ("ssm_b_re", (64, 64, 16)), ("ssm_b_im", (64, 64, 16)), ("ssm_c_re", (64, 16, 64)), ("ssm_c_im", (64, 16, 64)),
    ("ssm_d", (1024,)), ("ssm_glu_w", (1024, 1024)), ("ssm_glu_b", (1024,)), ("w_out", (2048, 2048)),
    ("ffn2_norm", (2048,)), ("ffn2_w_gate", (2048, 5632)), ("ffn2_w_up", (2048, 5632)), ("ffn2_w_down", (5632, 2048)),
    ("final_norm", (2048,)),
]


def build():
    nc = bass.Bass("TRN2", target_bir_lowering=False)
    I = {}
    for n, s in IN_SPECS:
        I[n] = nc.dram_tensor(n, list(s), F32, kind="ExternalInput").ap()
    Cd = {}
    for n, s, dt in CONST_SPECS:
        Cd[n] = nc.dram_tensor("c_" + n, list(s), dt, kind="ExternalInput").ap()
    out = nc.dram_tensor("out", [2048, 2048], F32, kind="ExternalOutput").ap()
    skind = "ExternalOutput" if DBG else "Internal"
    x1s = nc.dram_tensor("x1s", [16, 128, 2048], F32, kind=skind).ap()
    QT = nc.dram_tensor("QT", [16, 64, 2048], BF16, kind=skind).ap()
    KT = nc.dram_tensor("KT", [16, 64, 4096], BF16, kind=skind).ap()
    Vd = nc.dram_tensor("Vd", [4096, 1024], BF16, kind=skind).ap()
    uTd = nc.dram_tensor("uTd", [8, 128, 4096], BF16, kind=skind).ap()
    mixT = nc.dram_tensor("mixT", [16, 128, 2048], BF16, kind=skind).ap()

    with contextlib.ExitStack() as es:
        kb = KB(nc, es)
        ps = [T(es.enter_context(nc.psum_tensor(f"ps{i}", [128, 512], F32))) for i in range(8)]

        def nps():
            kb.psi = (kb.psi + 1) % 8
            return ps[kb.psi]

        cst = {}
        for n in ["ident_f", "ident_b", "ones_f", "ones_b", "e0"]:
            spec = [x for x in CONST_SPECS if x[0] == n][0]
            cst[n] = kb.sb("k_" + n, spec[1], spec[2])
            kb.dma("sp", cst[n].t[:], Cd[n][:], writes=[cst[n]])
        ident_f, ident_b, ones_f, ones_b, e0 = (cst[n] for n in ["ident_f", "ident_b", "ones_f", "ones_b", "e0"])
        gcols = kb.sb("gcols", [128, 3, 16], F32)
        with nc.allow_non_contiguous_dma(reason="tiny gain vectors"):
            for i, n in enumerate(["ffn1_norm", "mix_norm", "ffn2_norm"]):
                kb.dma("sp", gcols.t[:, i, :], I[n].rearrange("(t p) -> p t", p=128), writes=[gcols])

        def evac(eng, out_ap, in_ap, reads, writes):
            if eng == "act":
                return kb.op("act", lambda: nc.scalar.copy(out=out_ap, in_=in_ap), reads=reads, writes=writes)
            return kb.op("dve", lambda: nc.vector.tensor_copy(out=out_ap, in_=in_ap), reads=reads, writes=writes)

        def rmsnorm_fm(xT, hT, gi, Tn, sq, rstd, tmp):
            nh = Tn // 512
            pp = [nps() for _ in range(nh)]
            for dt in range(16):
                for hf in range(nh):
                    s_ = sq[(dt * nh + hf) % 2]
                    if hf % 2 == 0:
                        kb.op("act", lambda: nc.scalar.activation(out=s_.t[:], in_=xT.t[:, dt, hf * 512:(hf + 1) * 512],
                                                                  func=AF.Square), reads=[xT.bs[dt]], writes=[s_])
                    else:
                        kb.op("dve", lambda: nc.vector.tensor_tensor(out=s_.t[:], in0=xT.t[:, dt, hf * 512:(hf + 1) * 512],
                                                                     in1=xT.t[:, dt, hf * 512:(hf + 1) * 512], op=ALU.mult),
                              reads=[xT.bs[dt]], writes=[s_])
                    kb.op("pe", lambda: nc.tensor.matmul(pp[hf].t[:], lhsT=ones_b.t[:], rhs=s_.t[:],
                                                         start=(dt == 0), stop=(dt == 15)),
                          reads=[s_, ones_b], writes=[pp[hf]])
            for hf in range(nh):
                kb.op("act", lambda: nc.scalar.activation(out=tmp.t[:, hf * 512:(hf + 1) * 512], in_=pp[hf].t[:],
                                                          func=AF.Ln, scale=1.0 / D, bias=1e-6),
                      reads=[pp[hf]], writes=[tmp])
            kb.op("act", lambda: nc.scalar.activation(out=rstd.t[:, :Tn], in_=tmp.t[:, :Tn], func=AF.Exp, scale=-0.5),
                  reads=[tmp], writes=[rstd])
            if hT is None:
                return
            for dt in range(16):
                kb.op("dve", lambda: nc.vector.scalar_tensor_tensor(
                    out=hT.t[:, dt, :Tn], in0=xT.t[:, dt, :Tn], scalar=gcols.t[:, gi, dt:dt + 1], in1=rstd.t[:, :Tn],
                    op0=ALU.mult, op1=ALU.mult), reads=[xT.bs[dt], rstd, gcols], writes=[hT])

        def ffn(xT, hT, wg, wu, wd, ph):
            aT = kb.sb("aT", [128, 12, 1024], BF16, ph, nb=12)
            gsl = [kb.sb(f"gsl{i}", [128, 16, 256], BF16, ph) for i in range(2)]
            usl = [kb.sb(f"usl{i}", [128, 16, 256], BF16, ph) for i in range(2)]
            dsl = [kb.sb(f"dsl{i}", [128, 12, 256], BF16, ph) for i in range(2)]
            sg = [kb.sb(f"sg{i}", [128, 512], F32, ph) for i in range(2)]
            wgv = wg.rearrange("(k p) n -> p k n", p=128)
            wuv = wu.rearrange("(k p) n -> p k n", p=128)
            wdv = wd.rearrange("(f p) n -> p f n", p=128)
            si = 0
            di = 0
            gi_ = 0
            for (f0, nf) in [(0, 12), (12, 12), (24, 10), (34, 10)]:
                for fp in range(nf // 2):
                    fa = f0 + 2 * fp
                    g_ = gsl[si % 2]
                    u_ = usl[si % 2]
                    si += 1
                    kb.dma("pool", g_.t[:], wgv[:, :, fa * 128:(fa + 2) * 128], writes=[g_])
                    kb.dma("pool", u_.t[:], wuv[:, :, fa * 128:(fa + 2) * 128], writes=[u_])
                    for j in range(2):
                        for hf in range(2):
                            pg = nps()
                            pu = nps()

                            def mm(p_, w_):
                                for k in range(16):
                                    ins = nc.tensor.matmul(p_.t[:], lhsT=w_.t[:, k, j * 128:(j + 1) * 128],
                                                           rhs=hT.t[:, k, hf * 512:(hf + 1) * 512],
                                                           start=(k == 0), stop=(k == 15))
                                return ins
                            kb.op("pe", lambda: mm(pg, g_), reads=[g_, hT], writes=[pg])
                            kb.op("pe", lambda: mm(pu, u_), reads=[u_, hT], writes=[pu])
                            s_ = sg[gi_ % 2]
                            gi_ += 1
                            kb.op("act", lambda: nc.scalar.activation(out=s_.t[:], in_=pg.t[:], func=AF.Silu),
                                  reads=[pg], writes=[s_])
                            ai = fa - f0 + j
                            kb.op("dve", lambda: nc.vector.tensor_tensor(out=aT.t[:, ai, hf * 512:(hf + 1) * 512],
                                                                         in0=s_.t[:], in1=pu.t[:], op=ALU.mult),
                                  reads=[s_, pu], writes=[aT.bs[ai]])
                for dp in range(8):
                    d_ = dsl[di % 2]
                    di += 1
                    kb.dma("pool", d_.t[:, :nf, :], wdv[:, f0:f0 + nf, dp * 256:(dp + 1) * 256], writes=[d_])
                    for j in range(2):
                        dt = dp * 2 + j
                        for hf in range(2):
                            py = nps()

                            def mm2():
                                for fi in range(nf):
                                    ins = nc.tensor.matmul(py.t[:], lhsT=d_.t[:, fi, j * 128:(j + 1) * 128],
                                                           rhs=aT.t[:, fi, hf * 512:(hf + 1) * 512],
                                                           start=(fi == 0), stop=(fi == nf - 1))
                                return ins
                            kb.op("pe", mm2, reads=[d_] + aT.bs[:nf], writes=[py])
                            kb.op("dve", lambda: nc.vector.scalar_tensor_tensor(
                                out=xT.t[:, dt, hf * 512:(hf + 1) * 512], in0=py.t[:], scalar=0.5,
                                in1=xT.t[:, dt, hf * 512:(hf + 1) * 512], op0=ALU.mult, op1=ALU.add),
                                reads=[py, xT.bs[dt]], writes=[xT.bs[dt]])

        def phase_A():
            w_in_v = I["w_in"].rearrange("(k p) n -> p k n", p=128)
            with contextlib.ExitStack() as pa:
                xT = kb.sb("xT", [128, 16, 1024], F32, pa, nb=16)
                hT = kb.sb("hT", [128, 16, 1024], BF16, pa)
                rstd = kb.sb("rstd", [128, 1024], F32, pa)
                tmp = kb.sb("ntmp", [128, 1024], F32, pa)
                sq = [kb.sb(f"sq{i}", [128, 512], BF16, pa) for i in range(2)]
                for tt in range(4):
                    with contextlib.ExitStack() as p1:
                        xin = [kb.sb(f"xin{i}", [128, 2048], F32, p1) for i in range(2)]
                        ei = 0
                        for bl in range(8):
                            xi = xin[bl % 2]
                            r0 = tt * 1024 + bl * 128
                            kb.dma("sp", xi.t[:], I["xl"][r0:r0 + 128, :], writes=[xi])
                            for q4 in range(4):
                                p_ = nps()

                                def tr():
                                    for j in range(4):
                                        dt = q4 * 4 + j
                                        ins = nc.tensor.transpose(out=p_.t[:, j * 128:(j + 1) * 128],
                                                                  in_=xi.t[:, dt * 128:(dt + 1) * 128],
                                                                  identity=ident_f.t[:])
                                    return ins
                                kb.op("pe", tr, reads=[xi, ident_f], writes=[p_])
                                evac("act" if ei % 2 else "dve",
                                     xT.t[:, q4 * 4:(q4 + 1) * 4, bl * 128:(bl + 1) * 128],
                                     p_.t[:].rearrange("p (a b) -> p a b", a=4), [p_],
                                     [xT.bs[q4 * 4 + j] for j in range(4)])
                                ei += 1
                        rmsnorm_fm(xT, hT, 0, 1024, sq, rstd, tmp)
                        ffn(xT, hT, I["ffn1_w_gate"], I["ffn1_w_up"], I["ffn1_w_down"], p1)
                        kb.dma("sp", x1s.rearrange("t p n -> p t n")[:, :, tt * 512:(tt + 1) * 512], xT.t[:, :, 0:512],
                               reads=[xT])
                        rmsnorm_fm(xT, hT, 1, 1024, sq, rstd, tmp)
                        kb.barrier()
                    with contextlib.ExitStack() as p2:
                        wsl = [kb.sb(f"wsl{i}", [128, 16, 256], BF16, p2) for i in range(2)]
                        qst = [kb.sb(f"qst{i}", [128, 2, 1024], BF16, p2) for i in range(2)]
                        vst = [kb.sb(f"vst{i}", [128, 8, 256], BF16, p2) for i in range(2)]
                        ust = [kb.sb(f"ust{i}", [128, 2, 1024], BF16, p2) for i in range(2)]
                        wi = 0
                        ei = 0
                        for part, base, dst, ntok in (("q", 0, QT, 512), ("k", 1024, KT, 1024)):
                            for sl in range(4):
                                w_ = wsl[wi % 2]
                                st_ = qst[wi % 2]
                                wi += 1
                                kb.dma("pool", w_.t[:], w_in_v[:, :, base + sl * 256: base + (sl + 1) * 256], writes=[w_])
                                for jp in range(2):
                                    for hf in range(ntok // 512):
                                        p_ = nps()

                                        def mm():
                                            for k in range(16):
                                                ins = nc.tensor.matmul(p_.t[:], lhsT=w_.t[:, k, jp * 128:(jp + 1) * 128],
                                                                       rhs=hT.t[:, k, hf * 512:(hf + 1) * 512],
                                                                       start=(k == 0), stop=(k == 15))
                                            return ins
                                        kb.op("pe", mm, reads=[w_, hT], writes=[p_])
                                        evac("act" if ei % 2 else "dve", st_.t[:, jp, hf * 512:(hf + 1) * 512],
                                             p_.t[:], [p_], [st_])
                                        ei += 1
                                c0_, c1_ = (tt * 512, tt * 512 + 512) if part == "q" else (tt * 1024, tt * 1024 + 1024)
                                kb.dma("sp", dst[sl * 4:(sl + 1) * 4, :, c0_:c1_].rearrange("(jp e) p n -> (e p) jp n", e=2),
                                       st_.t[:, :, 0:ntok], reads=[st_])
                        for sl in range(4):
                            w_ = wsl[wi % 2]
                            st_ = vst[wi % 2]
                            wi += 1
                            kb.dma("pool", w_.t[:], w_in_v[:, :, 2048 + sl * 256: 2048 + (sl + 1) * 256], writes=[w_])
                            for bl in range(8):
                                p_ = nps()

                                def mm():
                                    for k in range(16):
                                        ins = nc.tensor.matmul(p_.t[:, 0:256], lhsT=hT.t[:, k, bl * 128:(bl + 1) * 128],
                                                               rhs=w_.t[:, k, :], start=(k == 0), stop=(k == 15))
                                    return ins
                                kb.op("pe", mm, reads=[w_, hT], writes=[p_])
                                evac("act" if ei % 2 else "dve", st_.t[:, bl, :], p_.t[:, 0:256], [p_], [st_])
                                ei += 1
                            kb.dma("sp", Vd[tt * 1024:(tt + 1) * 1024, sl * 256:(sl + 1) * 256].rearrange("(b p) n -> p b n", p=128),
                                   st_.t[:], reads=[st_])
                        for sl in range(4):
                            w_ = wsl[wi % 2]
                            st_ = ust[wi % 2]
                            wi += 1
                            kb.dma("pool", w_.t[:], w_in_v[:, :, 3072 + sl * 256: 3072 + (sl + 1) * 256], writes=[w_])
                            for j in range(2):
                                for hf in range(2):
                                    p_ = nps()

                                    def mm():
                                        for k in range(16):
                                            ins = nc.tensor.matmul(p_.t[:], lhsT=w_.t[:, k, j * 128:(j + 1) * 128],
                                                                   rhs=hT.t[:, k, hf * 512:(hf + 1) * 512],
                                                                   start=(k == 0), stop=(k == 15))
                                        return ins
                                    kb.op("pe", mm, reads=[w_, hT], writes=[p_])
                                    evac("act" if ei % 2 else "dve", st_.t[:, j, hf * 512:(hf + 1) * 512], p_.t[:], [p_], [st_])
                                    ei += 1
                            kb.dma("sp", uTd[sl * 2:(sl + 1) * 2, :, tt * 1024:(tt + 1) * 1024].rearrange("t p n -> p t n"),
                                   st_.t[:], reads=[st_])
                        kb.barrier()

        def phase_C():
            w_out_v = I["w_out"].rearrange("(k p) n -> p k n", p=128)
            with contextlib.ExitStack() as pc:
                xT = kb.sb("xT", [128, 16, 1024], F32, pc, nb=16)
                hT = kb.sb("hT", [128, 16, 1024], BF16, pc)
                rstd = kb.sb("rstd", [128, 1024], F32, pc)
                tmp = kb.sb("ntmp", [128, 1024], F32, pc)
                sq = [kb.sb(f"sq{i}", [128, 512], BF16, pc) for i in range(2)]
                gfin = kb.sb("gfin", [128, 2048], F32, pc)
                rcol = kb.sb("rcol", [128, 8], F32, pc)
                kb.dma("sp", gfin.t[:], I["final_norm"].partition_broadcast(128), writes=[gfin])
                for ot in range(2):
                    c0 = ot * 1024
                    kb.dma("sp", hT.t[:], mixT.rearrange("t p n -> p t n")[:, :, c0:c0 + 1024], writes=[hT])
                    for q4 in range(4):
                        kb.dma("sp", xT.t[:, q4 * 4:(q4 + 1) * 4, :],
                               x1s.rearrange("t p n -> p t n")[:, q4 * 4:(q4 + 1) * 4, c0:c0 + 1024],
                               writes=[xT.bs[q4 * 4 + j] for j in range(4)])
                    with contextlib.ExitStack() as p1:
                        wsl = [kb.sb(f"wsl{i}", [128, 16, 256], BF16, p1) for i in range(2)]
                        for sl in range(8):
                            w_ = wsl[sl % 2]
                            kb.dma("pool", w_.t[:], w_out_v[:, :, sl * 256:(sl + 1) * 256], writes=[w_])
                            for j in range(2):
                                dt = sl * 2 + j
                                for hf in range(2):
                                    p_ = nps()

                                    def mm():
                                        for k in range(16):
                                            ins = nc.tensor.matmul(p_.t[:], lhsT=w_.t[:, k, j * 128:(j + 1) * 128],
                                                                   rhs=hT.t[:, k, hf * 512:(hf + 1) * 512],
                                                                   start=(k == 0), stop=(k == 15))
                                        return ins
                                    kb.op("pe", mm, reads=[w_, hT], writes=[p_])
                                    kb.op("dve", lambda: nc.vector.tensor_tensor(
                                        out=xT.t[:, dt, hf * 512:(hf + 1) * 512], in0=p_.t[:],
                                        in1=xT.t[:, dt, hf * 512:(hf + 1) * 512], op=ALU.add),
                                        reads=[p_, xT.bs[dt]], writes=[xT.bs[dt]])
                        rmsnorm_fm(xT, hT, 2, 1024, sq, rstd, tmp)
                        ffn(xT, hT, I["ffn2_w_gate"], I["ffn2_w_up"], I["ffn2_w_down"], p1)
                        kb.barrier()
                    with contextlib.ExitStack() as p2:
                        ostb = [kb.sb(f"ost{i}", [128, 2048], F32, p2) for i in range(2)]
                        rmsnorm_fm(xT, None, 0, 1024, sq, rstd, tmp)
                        p_ = nps()

                        def mmr():
                            for bl in range(8):
                                ins = nc.tensor.matmul(p_.t[:, bl:bl + 1], lhsT=rstd.t[:, bl * 128:(bl + 1) * 128],
                                                       rhs=e0.t[:, 0:1], start=True, stop=True)
                            return ins
                        kb.op("pe", mmr, reads=[rstd, e0], writes=[p_])
                        evac("dve", rcol.t[:], p_.t[:, 0:8], [p_], [rcol])
                        for bl in range(8):
                            ost = ostb[bl % 2]
                            for q4 in range(4):
                                p_ = nps()

                                def tr():
                                    for j in range(4):
                                        dt = q4 * 4 + j
                                        ins = nc.tensor.transpose(out=p_.t[:, j * 128:(j + 1) * 128],
                                                                  in_=xT.t[:, dt, bl * 128:(bl + 1) * 128],
                                                                  identity=ident_f.t[:])
                                    return ins
                                kb.op("pe", tr, reads=[xT, ident_f], writes=[p_])
                                kb.op("dve", lambda: nc.vector.scalar_tensor_tensor(
                                    out=ost.t[:, q4 * 512:(q4 + 1) * 512], in0=p_.t[:], scalar=rcol.t[:, bl:bl + 1],
                                    in1=gfin.t[:, q4 * 512:(q4 + 1) * 512], op0=ALU.mult, op1=ALU.mult),
                                    reads=[p_, rcol, gfin], writes=[ost])
                            kb.dma("sp", out[c0 + bl * 128:c0 + (bl + 1) * 128, :], ost.t[:], reads=[ost])
                        kb.barrier()

        def phase_T():
            with contextlib.ExitStack() as pt_:
                QTa = [[kb.sb(f"qta{s}{c}", [66, 2048], BF16, pt_, nb=2) for c in range(2)] for s in range(2)]
                KTa = [[kb.sb(f"kta{s}{c}", [66, 4096], BF16, pt_, nb=2) for c in range(2)] for s in range(2)]
                Vh = [kb.sb(f"vh{s}", [128, 32, 128], BF16, pt_) for s in range(2)]
                mask = kb.sb("mask", [128, 8, 512], BF16, pt_)
                abias = kb.sb("abias", [128, 640], F32, pt_)
                pT = [kb.sb(f"pT{i}", [128, 512], BF16, pt_) for i in range(6)]
                Lacc = [kb.sb(f"Lacc{i}", [128, 512], F32, pt_) for i in range(2)]
                f1 = kb.sb("f1", [128, 512], F32, pt_)
                f2 = kb.sb("f2", [128, 512], F32, pt_)
                f3 = kb.sb("f3", [128, 512], F32, pt_)
                f4 = kb.sb("f4", [128, 512], F32, pt_)
                f5 = kb.sb("f5", [128, 512], BF16, pt_)
                aost = [kb.sb(f"aost{i}", [128, 512], BF16, pt_) for i in range(2)]
                lamb = kb.sb("lamb", [128, 4, 64], F32, pt_)
                lsm = kb.sb("lsm", [128, 8], F32, pt_)
                kb.dma("sp", mask.t[:], Cd["maskadd"][:], writes=[mask])
                kb.dma("sp", abias.t[:], Cd["abias"][:], writes=[abias])
                for i, n in enumerate(["lambda_q1", "lambda_k1", "lambda_q2", "lambda_k2"]):
                    kb.dma("sp", lamb.t[:, i, :], I[n].partition_broadcast(128), writes=[lamb])
                with nc.allow_non_contiguous_dma(reason="tiny"):
                    kb.dma("sp", lsm.t[:, 7:8], I["attn_subln"].rearrange("(p o) -> p o", o=1), writes=[lsm])
                kb.op("dve", lambda: nc.vector.tensor_tensor(out=lamb.t[:, 0, :], in0=lamb.t[:, 0, :], in1=lamb.t[:, 1, :], op=ALU.mult),
                      reads=[lamb], writes=[lamb])
                kb.op("dve", lambda: nc.vector.tensor_tensor(out=lamb.t[:, 2, :], in0=lamb.t[:, 2, :], in1=lamb.t[:, 3, :], op=ALU.mult),
                      reads=[lamb], writes=[lamb])
                kb.op("dve", lambda: nc.vector.reduce_sum(out=lsm.t[:, 0:1], in_=lamb.t[:, 0, :], axis=AX.X), reads=[lamb], writes=[lsm])
                kb.op("dve", lambda: nc.vector.reduce_sum(out=lsm.t[:, 1:2], in_=lamb.t[:, 2, :], axis=AX.X), reads=[lamb], writes=[lsm])
                kb.op("act", lambda: nc.scalar.activation(out=lsm.t[:, 2:4], in_=lsm.t[:, 0:2], func=AF.Exp), reads=[lsm], writes=[lsm])
                kb.op("dve", lambda: nc.vector.scalar_tensor_tensor(out=lsm.t[:, 4:5], in0=lsm.t[:, 3:4], scalar=-0.2,
                                                                    in1=lsm.t[:, 2:3], op0=ALU.add, op1=ALU.subtract),
                      reads=[lsm], writes=[lsm])
                kb.op("dve", lambda: nc.vector.tensor_scalar(out=lsm.t[:, 5:6], in0=lsm.t[:, 7:8], scalar1=0.8, scalar2=None,
                                                             op0=ALU.mult), reads=[lsm], writes=[lsm])
                for s in range(2):
                    for c in range(2):
                        kb.op("dve", lambda: nc.vector.memset(KTa[s][c].t[64:66, :], 1.0), writes=[KTa[s][c].bs[1]])
                Vv = Vd.rearrange("(b p) d -> p b d", p=128)
                sidx = 0
                pidx = 0
                oi = 0
                for h in range(8):
                    st = h % 2
                    for c in range(2):
                        kb.dma("sp", QTa[st][c].t[0:64, :], QT[h * 2 + c], writes=[QTa[st][c].bs[0]])
                        kb.dma("sp", QTa[st][c].t[64:66, :], Cd["qaug"][2 * h:2 * h + 2, :], writes=[QTa[st][c].bs[1]])
                        kb.dma("sp", KTa[st][c].t[0:64, :], KT[h * 2 + c], writes=[KTa[st][c].bs[0]])
                    kb.dma("sp", Vh[st].t[:], Vv[:, :, h * 128:(h + 1) * 128], writes=[Vh[st]])
                    for ci in range(4):
                        nkb = 8 * ci + 8
                        steps = [(k_, c) for k_ in range(nkb) for c in range(2)]
                        info = {}

                        def lo_of(k_):
                            return ((k_ - 8 * ci) % 4) * 128 if k_ >= 8 * ci else 0

                        def qk(step):
                            nonlocal sidx, pidx
                            k_, c = step
                            tail = k_ >= 8 * ci
                            lo = lo_of(k_)
                            p_ = ps[4 + sidx % 4]
                            sidx += 1
                            pt = pT[pidx % 6]
                            pidx += 1

                            def f():
                                ins = nc.tensor.matmul(p_.t[:, lo:512], lhsT=KTa[st][c].t[0:66, k_ * 128:(k_ + 1) * 128],
                                                       rhs=QTa[st][c].t[0:66, ci * 512 + lo:(ci + 1) * 512],
                                                       start=True, stop=not tail)
                                if tail:
                                    ins = nc.tensor.matmul(p_.t[:, lo:512], lhsT=ident_b.t[:], rhs=mask.t[:, k_ - 8 * ci, lo:512],
                                                           start=False, stop=True)
                                return ins
                            kb.op("pe", f, reads=[KTa[st][c], QTa[st][c], mask, ident_b], writes=[p_])
                            col = h * 80 + _OFF[ci] + k_
                            kb.op("act", lambda: nc.scalar.activation(out=pt.t[:, lo:512], in_=p_.t[:, lo:512], func=AF.Exp,
                                                                      bias=abias.t[:, col:col + 1], scale=0.125),
                                  reads=[p_, abias], writes=[pt])
                            info[step] = pt

                        def pv(step):
                            k_, c = step
                            pt = info[step]
                            lo = lo_of(k_)
                            if k_ % 2 == 0:
                                def g():
                                    nc.tensor.matmul(ps[c].t[:, lo:512], lhsT=Vh[st].t[:, k_, :], rhs=pt.t[:, lo:512],
                                                     start=(k_ == 0), stop=(k_ == nkb - 1))
                                    return nc.tensor.matmul(ps[2 + c].t[:, lo:512], lhsT=ones_b.t[:], rhs=pt.t[:, lo:512],
                                                            start=(k_ == 0), stop=False)
                                kb.op("pe", g, reads=[Vh[st], pt, ones_b], writes=[ps[c], ps[2 + c]])
                            else:
                                kb.op("pe", lambda: nc.tensor.matmul(ps[c].t[:, lo:512], lhsT=Vh[st].t[:, k_, :], rhs=pt.t[:, lo:512],
                                                                     start=False, stop=(k_ == nkb - 1)),
                                      reads=[Vh[st], pt], writes=[ps[c]])
                                kb.op("dve", lambda: nc.vector.tensor_tensor(out=Lacc[c].t[:, lo:512], in0=Lacc[c].t[:, lo:512],
                                                                             in1=pt.t[:, lo:512], op=ALU.add),
                                      reads=[pt, Lacc[c]], writes=[Lacc[c]])
                        for c in range(2):
                            kb.op("pool", lambda: nc.gpsimd.memset(Lacc[c].t[:], 0.0), writes=[Lacc[c]])
                        LOOK = 3
                        for i in range(len(steps) + LOOK):
                            if i < len(steps):
                                qk(steps[i])
                            if i >= LOOK:
                                pv(steps[i - LOOK])
                        pl_ = []
                        for c in range(2):
                            kb.op("pe", lambda: nc.tensor.matmul(ps[2 + c].t[:], lhsT=ones_f.t[:], rhs=Lacc[c].t[:], start=False, stop=True),
                                  reads=[Lacc[c], ones_f], writes=[ps[2 + c]])
                            pl_.append(ps[2 + c])
                        kb.op("dve", lambda: nc.vector.tensor_copy(out=f2.t[:], in_=ps[0].t[:]), reads=[ps[0]], writes=[f2])
                        kb.op("act", lambda: nc.scalar.activation(out=f1.t[:], in_=pl_[0].t[:], func=AF.Ln), reads=[pl_[0]], writes=[f1])
                        kb.op("dve", lambda: nc.vector.tensor_copy(out=f3.t[:], in_=ps[1].t[:]), reads=[ps[1]], writes=[f3])
                        kb.op("act", lambda: nc.scalar.activation(out=f4.t[:], in_=pl_[1].t[:], func=AF.Ln), reads=[pl_[1]], writes=[f4])
                        kb.op("act", lambda: nc.scalar.activation(out=f1.t[:], in_=f1.t[:], func=AF.Exp, scale=-1.0), reads=[f1], writes=[f1])
                        kb.op("act", lambda: nc.scalar.activation(out=f4.t[:], in_=f4.t[:], func=AF.Exp, scale=-1.0), reads=[f4], writes=[f4])
                        kb.op("dve", lambda: nc.vector.tensor_tensor(out=f2.t[:], in0=f2.t[:], in1=f1.t[:], op=ALU.mult),
                              reads=[f2, f1], writes=[f2])
                        kb.op("dve", lambda: nc.vector.tensor_tensor(out=f3.t[:], in0=f3.t[:], in1=f4.t[:], op=ALU.mult),
                              reads=[f3, f4], writes=[f3])
                        kb.op("dve", lambda: nc.vector.scalar_tensor_tensor(out=f2.t[:], in0=f3.t[:], scalar=lsm.t[:, 4:5],
                                                                            in1=f2.t[:], op0=ALU.mult, op1=ALU.add),
                              reads=[f3, f2, lsm], writes=[f2])
                        kb.op("dve", lambda: nc.vector.tensor_tensor(out=f5.t[:], in0=f2.t[:], in1=f2.t[:], op=ALU.mult),
                              reads=[f2], writes=[f5])
                        p_ = ps[4 + sidx % 4]
                        sidx += 1
                        kb.op("pe", lambda: nc.tensor.matmul(p_.t[:], lhsT=ones_b.t[:], rhs=f5.t[:], start=True, stop=True),
                              reads=[f5, ones_b], writes=[p_])
                        kb.op("act", lambda: nc.scalar.activation(out=f4.t[:], in_=p_.t[:], func=AF.Ln, scale=1.0 / 128, bias=1e-5),
                              reads=[p_], writes=[f4])
                        kb.op("act", lambda: nc.scalar.activation(out=f1.t[:], in_=f4.t[:], func=AF.Exp, scale=-0.5),
                              reads=[f4], writes=[f1])
                        ao = aost[oi % 2]
                        oi += 1
                        kb.op("dve", lambda: nc.vector.scalar_tensor_tensor(out=ao.t[:], in0=f2.t[:], scalar=lsm.t[:, 5:6],
                                                                            in1=f1.t[:], op0=ALU.mult, op1=ALU.mult),
                              reads=[f2, f1, lsm], writes=[ao])
                        kb.dma("sp", mixT[h, :, ci * 512:(ci + 1) * 512], ao.t[:], reads=[ao])
                kb.barrier()

        def phase_S():
            TWO_PI = 2.0 * PI

            def rr_sin(x_ap, shift, out_ap, t_ap, ti_ap, r_ap, rd, wr):
                kb.op("dve", lambda: nc.vector.tensor_scalar_add(out=r_ap, in0=x_ap, scalar1=float(shift)), reads=rd, writes=wr)
                kb.op("dve", lambda: nc.vector.tensor_scalar_mul(out=t_ap, in0=r_ap, scalar1=1.0 / TWO_PI), reads=wr, writes=wr)
                kb.op("dve", lambda: nc.vector.tensor_copy(out=ti_ap, in_=t_ap), reads=wr, writes=wr)
                kb.op("dve", lambda: nc.vector.tensor_copy(out=t_ap, in_=ti_ap), reads=wr, writes=wr)
                kb.op("dve", lambda: nc.vector.scalar_tensor_tensor(out=r_ap, in0=t_ap, scalar=-TWO_PI, in1=r_ap,
                                                                    op0=ALU.mult, op1=ALU.add), reads=wr, writes=wr)
                kb.op("dve", lambda: nc.vector.tensor_scalar(out=r_ap, in0=r_ap, scalar1=-PI, scalar2=PI,
                                                             op0=ALU.max, op1=ALU.min), reads=wr, writes=wr)
                if out_ap is not None:
                    kb.op("act", lambda: nc.scalar.activation(out=out_ap, in_=r_ap, func=AF.Sin), reads=wr, writes=wr)

            def tt(out_ap, a_ap, b_ap, op, rd, wr):
                kb.op("dve", lambda: nc.vector.tensor_tensor(out=out_ap, in0=a_ap, in1=b_ap, op=op), reads=rd, writes=wr)

            with contextlib.ExitStack() as pS:
                sel = kb.sb("sel", [128, 64, 128], BF16, pS)
                kb.dma("sp", sel.t[:], Cd["sel"][:], writes=[sel])
                kc = {}
                for n in ["maskT", "tp", "cs", "ss", "ssprev", "posp"]:
                    spec = [x for x in CONST_SPECS if x[0] == n][0]
                    kc[n] = kb.sb("k_" + n, spec[1], spec[2], pS)
                    kb.dma("sp", kc[n].t[:], Cd[n][:], writes=[kc[n]])
                gT = kb.sb("gT", [128, 8, 2048], BF16, pS, nb=8)
                drep = kb.sb("drep", [128, 64], F32, pS)
                G = kb.sb("plG", [64, 26, 64], F32, pS)
                Gi = kb.sb("plGi", [64, 64], I32, pS)
                Pw = kb.sb("Pw", [64, 16, 2, 64], F32, pS)
                Pinv = kb.sb("Pinv", [64, 8, 2, 64], F32, pS)
                bb = kb.sb("bb", [64, 2, 64, 16], F32, pS)
                CT = kb.sb("CT", [64, 2, 1024], F32, pS)
                p0 = contextlib.ExitStack()
                braw = kb.sb("braw", [64, 2, 64, 16], F32, p0)
                cin = kb.sb("cin", [128, 2, 8, 64], F32, p0)
                bt1 = kb.sb("bt1", [64, 64, 16], F32, p0)
                bt2 = kb.sb("bt2", [64, 64, 16], F32, p0)
                PL = [G, Gi, Pw, Pinv]
                with nc.allow_non_contiguous_dma(reason="tiny parameter tables"):
                    for j in range(8):
                        kb.dma("sp", drep.t[j * 16:(j + 1) * 16, :], I["ssm_d"].rearrange("(g h) -> h g", h=16), writes=[drep])
                    kb.dma("sp", G.t[:, 0, :], I["ssm_lambda_re"].rearrange("g p -> p g"), writes=[G])
                    kb.dma("sp", G.t[:, 1, :], I["ssm_lambda_im"].rearrange("g p -> p g"), writes=[G])
                kb.dma("sp", G.t[:, 2, :], I["ssm_log_dt"].partition_broadcast(64), writes=[G])
                kb.dma("sp", braw.t[:, 0], I["ssm_b_re"].rearrange("g p h -> p g h"), writes=[braw])
                kb.dma("sp", braw.t[:, 1], I["ssm_b_im"].rearrange("g p h -> p g h"), writes=[braw])
                kb.dma("sp", cin.t[:, 0], I["ssm_c_re"].rearrange("g h p -> (g h) p").rearrange("(t q) p -> q t p", q=128), writes=[cin])
                kb.dma("sp", cin.t[:, 1], I["ssm_c_im"].rearrange("g h p -> (g h) p").rearrange("(t q) p -> q t p", q=128), writes=[cin])
                g = lambda i: G.t[:, i, :]
                LR, LI, DT, LDR, LDI, MAG, MAGI, SIN, COS, ARE, AIM, IRE, IIM, DEN, AM1, FRE, FIM, T1, T2, R1, R2 = range(21)
                kb.op("act", lambda: nc.scalar.activation(out=g(DT), in_=g(DT), func=AF.Exp), reads=PL, writes=PL)
                tt(g(LDR), g(LR), g(DT), ALU.mult, PL, PL)
                tt(g(LDI), g(LI), g(DT), ALU.mult, PL, PL)
                kb.op("act", lambda: nc.scalar.activation(out=g(MAG), in_=g(LDR), func=AF.Exp), reads=PL, writes=PL)
                kb.op("act", lambda: nc.scalar.activation(out=g(MAGI), in_=g(LDR), func=AF.Exp, scale=-1.0), reads=PL, writes=PL)
                rr_sin(g(LDI), 0.0, g(SIN), g(R1), Gi.t[:], g(R2), PL, PL)
                rr_sin(g(LDI), PI / 2, g(COS), g(R1), Gi.t[:], g(R2), PL, PL)
                tt(g(ARE), g(MAG), g(COS), ALU.mult, PL, PL)
                tt(g(AIM), g(MAG), g(SIN), ALU.mult, PL, PL)
                tt(g(IRE), g(MAGI), g(COS), ALU.mult, PL, PL)
                kb.op("dve", lambda: nc.vector.scalar_tensor_tensor(out=g(IIM), in0=g(MAGI), scalar=-1.0, in1=g(SIN),
                                                                    op0=ALU.mult, op1=ALU.mult), reads=PL, writes=PL)
                tt(g(T1), g(LR), g(LR), ALU.mult, PL, PL)
                tt(g(T2), g(LI), g(LI), ALU.mult, PL, PL)
                tt(g(DEN), g(T1), g(T2), ALU.add, PL, PL)
                kb.op("dve", lambda: nc.vector.reciprocal(out=g(DEN), in_=g(DEN)), reads=PL, writes=PL)
                kb.op("dve", lambda: nc.vector.tensor_scalar_add(out=g(AM1), in0=g(ARE), scalar1=-1.0), reads=PL, writes=PL)
                tt(g(T1), g(AM1), g(LR), ALU.mult, PL, PL)
                tt(g(T2), g(AIM), g(LI), ALU.mult, PL, PL)
                tt(g(T1), g(T1), g(T2), ALU.add, PL, PL)
                tt(g(FRE), g(T1), g(DEN), ALU.mult, PL, PL)
                tt(g(T1), g(AIM), g(LR), ALU.mult, PL, PL)
                tt(g(T2), g(AM1), g(LI), ALU.mult, PL, PL)
                tt(g(T1), g(T1), g(T2), ALU.subtract, PL, PL)
                tt(g(FIM), g(T1), g(DEN), ALU.mult, PL, PL)

                def cpow(P_, n, xr, xi):
                    kb.op("dve", lambda: nc.vector.memset(P_.t[:, 0, 0, :], 1.0), reads=PL, writes=PL)
                    kb.op("dve", lambda: nc.vector.memset(P_.t[:, 0, 1, :], 0.0), reads=PL, writes=PL)
                    for j in range(1, n):
                        pr, pi = P_.t[:, j - 1, 0, :], P_.t[:, j - 1, 1, :]
                        tt(g(T1), pr, xr, ALU.mult, PL, PL)
                        tt(g(T2), pi, xi, ALU.mult, PL, PL)
                        tt(P_.t[:, j, 0, :], g(T1), g(T2), ALU.subtract, PL, PL)
                        tt(g(T1), pr, xi, ALU.mult, PL, PL)
                        tt(g(T2), pi, xr, ALU.mult, PL, PL)
                        tt(P_.t[:, j, 1, :], g(T1), g(T2), ALU.add, PL, PL)
                cpow(Pw, 16, g(ARE), g(AIM))
                cpow(Pinv, 8, g(IRE), g(IIM))
                fb = lambda i: G.t[:, i, :].unsqueeze(2).to_broadcast([64, 64, 16])
                PB = PL + [bb, braw, bt1, bt2]
                tt(bt1.t[:], fb(FRE), braw.t[:, 0], ALU.mult, PB, PB)
                tt(bt2.t[:], fb(FIM), braw.t[:, 1], ALU.mult, PB, PB)
                tt(bb.t[:, 0], bt1.t[:], bt2.t[:], ALU.subtract, PB, PB)
                tt(bt1.t[:], fb(FRE), braw.t[:, 1], ALU.mult, PB, PB)
                tt(bt2.t[:], fb(FIM), braw.t[:, 0], ALU.mult, PB, PB)
                tt(bb.t[:, 1], bt1.t[:], bt2.t[:], ALU.add, PB, PB)
                for part in range(2):
                    for half in range(2):
                        p_ = nps()

                        def trc():
                            for t4 in range(4):
                                ins = nc.tensor.transpose(out=p_.t[0:64, t4 * 128:(t4 + 1) * 128],
                                                          in_=cin.t[:, part, half * 4 + t4, :], identity=ident_f.t[:])
                            return ins
                        kb.op("pe", trc, reads=[cin, ident_f], writes=[p_])
                        evac("dve", CT.t[:, part, half * 512:(half + 1) * 512], p_.t[0:64, :], [p_], [CT])

                kb.barrier()
                p0.close()
                PB = PL + [bb]
                pb = contextlib.ExitStack()
                BjL = [kb.sb(f"Bj{i}", [64, 2, 8, 8, 16], BF16, pb) for i in range(1)]
                CjL = [kb.sb(f"Cj{i}", [64, 2, 8, 8, 16], BF16, pb) for i in range(1)]
                c1 = kb.sb("c1", [64, 8, 8, 16], F32, pb)
                c2 = kb.sb("c2", [64, 8, 8, 16], F32, pb)
                TgL = [kb.sb(f"Tg{i}", [128, 8, 128], BF16, pb) for i in range(1)]
                BsTL = [kb.sb(f"BsT{i}", [128, 8, 128], BF16, pb) for i in range(1)]
                Tm = kb.sb("Tm", [128, 128], F32, pb)
                ut = kb.sb("ut", [128, 4096], BF16, pb)
                U8 = kb.sb("U8", [128, 8, 512], BF16, pb)
                Sp = kb.sb("Sp", [128, 4, 8, 128], F32, pb, nb=4)
                tb = kb.sb("tb", [128, 4, 8, 64], F32, pb)
                tw = kb.sb("tw", [128, 6, 512], F32, pb)
                twi = kb.sb("twi", [128, 512], I32, pb)
                ldtb = kb.sb("ldtb", [128, 8], F32, pb)
                Z = kb.sb("Z", [128, 8, 2, 64], F32, pb)
                Wsb = kb.sb("Wsb", [128, 8, 2, 64], F32, pb)
                z1 = kb.sb("z1", [128, 8, 64], F32, pb)
                z2 = kb.sb("z2", [128, 8, 64], F32, pb)
                xiTL = [kb.sb(f"xiT{i}", [64, 2, 256], BF16, pb) for i in range(2)]
                xb = kb.sb("xb", [128, 4, 8, 128], BF16, pb, nb=4)
                ssb = kb.sb("ssb", [128, 2, 64], BF16, pb)
                kb.op("dve", lambda: nc.vector.tensor_copy(out=ssb.t[:, 0, :], in_=kc["ss"].t[:]), reads=[kc["ss"]], writes=[ssb])
                kb.op("dve", lambda: nc.vector.tensor_copy(out=ssb.t[:, 1, :], in_=kc["ssprev"].t[:]), reads=[kc["ssprev"]], writes=[ssb])
                ysbL = [kb.sb(f"ysb{i}", [128, 256], F32, pb) for i in range(2)]
                wqL = [kb.sb(f"wq{i}", [128, 256], F32, pb) for i in range(2)]
                sgmL = [kb.sb(f"sgm{i}", [128, 256], F32, pb) for i in range(2)]
                GsL = [kb.sb(f"Gs{i}", [128, 8, 256], BF16, pb) for i in range(1)]
                utr = kb.sb("utr", [128, 8, 512], BF16, pb)
                for bt in range(8):
                    g0 = 8 * bt
                    if True:
                        Bj, Cj, Tg, BsT, Gs = BjL[0], CjL[0], TgL[0], BsTL[0], GsL[0]
                        kb.dma("sp", ut.t[:], uTd[bt], writes=[ut])
                        kb.dma("sp", tw.t[:, 0, :], I["ssm_lambda_re"][g0:g0 + 8, :].rearrange("g p -> (g p)").partition_broadcast(128), writes=[tw])
                        kb.dma("sp", tw.t[:, 1, :], I["ssm_lambda_im"][g0:g0 + 8, :].rearrange("g p -> (g p)").partition_broadcast(128), writes=[tw])
                        kb.dma("sp", ldtb.t[:], I["ssm_log_dt"][g0:g0 + 8].partition_broadcast(128), writes=[ldtb])
                        PBJ = PB + [Bj, Cj, c1, c2, CT]
                        bc = lambda ap: ap.unsqueeze(2).to_broadcast([64, 8, 16])
                        b4 = lambda ap: ap.rearrange("p j g -> p g j").unsqueeze(3).to_broadcast([64, 8, 8, 16])
                        v4 = lambda ap: ap.unsqueeze(2).to_broadcast([64, 8, 8, 16])
                        pr, pi = b4(Pinv.t[:, :, 0, g0:g0 + 8]), b4(Pinv.t[:, :, 1, g0:g0 + 8])
                        br, bi = v4(bb.t[:, 0, g0:g0 + 8, :]), v4(bb.t[:, 1, g0:g0 + 8, :])
                        tt(c1.t[:], pr, br, ALU.mult, PBJ, PBJ)
                        tt(c2.t[:], pi, bi, ALU.mult, PBJ, PBJ)
                        tt(Bj.t[:, 0], c1.t[:], c2.t[:], ALU.subtract, PBJ, PBJ)
                        tt(c1.t[:], pr, bi, ALU.mult, PBJ, PBJ)
                        tt(c2.t[:], pi, br, ALU.mult, PBJ, PBJ)
                        tt(Bj.t[:, 1], c1.t[:], c2.t[:], ALU.add, PBJ, PBJ)

                        def build_C(joff):
                            cr = v4(CT.t[:, 0, g0 * 16:(g0 + 8) * 16].rearrange("p (g h) -> p g h", h=16))
                            ci_ = v4(CT.t[:, 1, g0 * 16:(g0 + 8) * 16].rearrange("p (g h) -> p g h", h=16))
                            pr_, pi_ = b4(Pw.t[:, joff:joff + 8, 0, g0:g0 + 8]), b4(Pw.t[:, joff:joff + 8, 1, g0:g0 + 8])
                            tt(c1.t[:], pr_, cr, ALU.mult, PBJ, PBJ)
                            tt(c2.t[:], pi_, ci_, ALU.mult, PBJ, PBJ)
                            tt(Cj.t[:, 0], c1.t[:], c2.t[:], ALU.subtract, PBJ, PBJ)
                            tt(c1.t[:], pr_, ci_, ALU.mult, PBJ, PBJ)
                            tt(c2.t[:], pi_, cr, ALU.mult, PBJ, PBJ)
                            kb.op("dve", lambda: nc.vector.scalar_tensor_tensor(
                                out=Cj.t[:, 1], in0=c1.t[:], scalar=-1.0, in1=c2.t[:],
                                op0=ALU.mult, op1=ALU.subtract), reads=PBJ, writes=PBJ)
                        build_C(0)
                        fl = lambda ap: ap.rearrange("p a b -> p (a b)")
                        for gl in range(8):
                            p_ = nps()

                            def mT():
                                nc.tensor.matmul(p_.t[:, 0:128], lhsT=fl(Bj.t[:, 0, gl]), rhs=fl(Cj.t[:, 0, gl]), start=True, stop=False)
                                return nc.tensor.matmul(p_.t[:, 0:128], lhsT=fl(Bj.t[:, 1, gl]), rhs=fl(Cj.t[:, 1, gl]), start=False, stop=True)
                            kb.op("pe", mT, reads=[Bj, Cj], writes=[p_])
                            kb.op("dve", lambda: nc.vector.tensor_tensor(out=Tm.t[:], in0=p_.t[:, 0:128], in1=kc["maskT"].t[:], op=ALU.mult),
                                  reads=[p_, kc["maskT"]], writes=[Tm])
                            kb.op("dve", lambda: nc.vector.scalar_tensor_tensor(
                                out=Tg.t[:, gl, :], in0=ident_f.t[:], scalar=drep.t[:, g0 + gl:g0 + gl + 1], in1=Tm.t[:],
                                op0=ALU.mult, op1=ALU.add), reads=[Tm, ident_f, drep], writes=[Tg])
                            p2 = nps()

                            def mB():
                                nc.tensor.matmul(p2.t[:, 0:64], lhsT=fl(Bj.t[:, 0, gl]), rhs=ident_b.t[0:64, 0:64], start=True, stop=True)
                                return nc.tensor.matmul(p2.t[:, 64:128], lhsT=fl(Bj.t[:, 1, gl]), rhs=ident_b.t[0:64, 0:64], start=True, stop=True)
                            kb.op("pe", mB, reads=[Bj, ident_b], writes=[p2])
                            evac("act", BsT.t[:, gl, :], p2.t[:, 0:128], [p2], [BsT])
                        build_C(8)
                        kb.op("act", lambda: nc.scalar.copy(out=utr.t[:], in_=ut.t[:].rearrange("p (c j) -> p j c", j=8)),
                              reads=[ut], writes=[utr])
                        for gl in range(8):
                            p_ = nps()

                            def mU():
                                for j in range(8):
                                    ins = nc.tensor.matmul(p_.t[:], lhsT=sel.t[:, gl * 8 + j, :], rhs=utr.t[:, j, :],
                                                           start=(j == 0), stop=(j == 7))
                                return ins
                            kb.op("pe", mU, reads=[sel, utr], writes=[p_])
                            evac("act", U8.t[:, gl, :], p_.t[:], [p_], [U8])
                            p3 = nps()

                            def mS():
                                for sb_ in range(4):
                                    ins = nc.tensor.matmul(p3.t[:, sb_ * 128:(sb_ + 1) * 128],
                                                           lhsT=U8.t[:, gl, sb_ * 128:(sb_ + 1) * 128], rhs=BsT.t[:, gl, :],
                                                           start=True, stop=True)
                                return ins
                            kb.op("pe", mS, reads=[U8, BsT], writes=[p3])
                            evac("act", Sp.t[:, :, gl, :], p3.t[:].rearrange("p (s n) -> p s n", s=4), [p3], [Sp])
                        TB = [tb, tw, twi, ldtb, kc["posp"]]
                        w_ = lambda i: tw.t[:, i, :]
                        w3 = lambda i: tw.t[:, i, :].rearrange("p (g q) -> p g q", q=64)
                        kb.op("act", lambda: nc.scalar.activation(out=ldtb.t[:], in_=ldtb.t[:], func=AF.Exp), reads=TB, writes=TB)
                        dtb = ldtb.t[:].unsqueeze(2).to_broadcast([128, 8, 64])
                        tt(w3(0), w3(0), dtb, ALU.mult, TB, TB)
                        tt(w3(1), w3(1), dtb, ALU.mult, TB, TB)
                        kb.op("act", lambda: nc.scalar.activation(out=w_(2), in_=w_(0), func=AF.Exp, scale=kc["posp"].t[:, 0:1]), reads=TB, writes=TB)
                        kb.op("act", lambda: nc.scalar.activation(out=w_(3), in_=w_(0), func=AF.Exp, scale=kc["posp"].t[:, 1:2]), reads=TB, writes=TB)
                        kb.op("dve", lambda: nc.vector.tensor_scalar_mul(out=w_(1), in0=w_(1), scalar1=8.0), reads=TB, writes=TB)
                        rr_sin(w_(1), 0.0, None, w_(4), twi.t[:], w_(5), TB, TB)
                        kb.op("dve", lambda: nc.vector.tensor_scalar_mul(out=w_(1), in0=w_(5), scalar1=kc["posp"].t[:, 2:3]), reads=TB, writes=TB)
                        rr_sin(w_(1), 0.0, w_(0), w_(4), twi.t[:], w_(5), TB, TB)
                        rr_sin(w_(1), PI / 2, w_(1), w_(4), twi.t[:], w_(5), TB, TB)
                        tbf = lambda i: tb.t[:, i].rearrange("p g q -> p (g q)")
                        tt(tbf(2), w_(2), w_(1), ALU.mult, TB, TB)
                        tt(tbf(3), w_(2), w_(0), ALU.mult, TB, TB)
                        tt(tbf(0), w_(3), w_(1), ALU.mult, TB, TB)
                        kb.op("dve", lambda: nc.vector.scalar_tensor_tensor(out=tbf(1), in0=w_(3), scalar=-1.0, in1=w_(0),
                                                                            op0=ALU.mult, op1=ALU.mult), reads=TB, writes=TB)
                        for sb_ in range(4):
                            sre, sim = Sp.t[:, sb_, :, 0:64], Sp.t[:, sb_, :, 64:128]
                            RZ = [Sp.bs[sb_], tb, z1, z2, Z]
                            tt(z1.t[:], sre, tb.t[:, 0], ALU.mult, RZ, [z1])
                            tt(z2.t[:], sim, tb.t[:, 1], ALU.mult, RZ, [z2])
                            tt(Z.t[:, :, 0, :], z1.t[:], z2.t[:], ALU.subtract, RZ, [Z])
                            tt(z1.t[:], sre, tb.t[:, 1], ALU.mult, RZ, [z1])
                            tt(z2.t[:], sim, tb.t[:, 0], ALU.mult, RZ, [z2])
                            tt(Z.t[:, :, 1, :], z1.t[:], z2.t[:], ALU.add, RZ, [Z])
                            for q in range(2):
                                p_ = nps()

                                def mW():
                                    ins = nc.tensor.matmul(p_.t[:], lhsT=kc["tp"].t[:],
                                                           rhs=Z.t[:, 4 * q:4 * q + 4].rearrange("p a b c -> p (a b c)"),
                                                           start=True, stop=(sb_ == 0))
                                    if sb_ > 0:
                                        ins = nc.tensor.matmul(p_.t[:], lhsT=kc["cs"].t[:],
                                                               rhs=Sp.t[:, sb_ - 1, 4 * q:4 * q + 4, :].rearrange("p a b -> p (a b)"),
                                                               start=False, stop=True)
                                    return ins
                                kb.op("pe", mW, reads=[Z, kc["tp"], kc["cs"]] + ([Sp.bs[sb_ - 1]] if sb_ > 0 else []), writes=[p_])
                                evac("act", Wsb.t[:, 4 * q:4 * q + 4].rearrange("p a b c -> p (a b c)"), p_.t[:], [p_], [Wsb])
                            wre, wim = Wsb.t[:, :, 0, :], Wsb.t[:, :, 1, :]
                            RX = [Wsb, tb, z1, z2, Sp.bs[sb_]]
                            tt(z1.t[:], wre, tb.t[:, 2], ALU.mult, RX, [z1])
                            tt(z2.t[:], wim, tb.t[:, 3], ALU.mult, RX, [z2])
                            tt(sre, z1.t[:], z2.t[:], ALU.subtract, RX, [Sp.bs[sb_]])
                            tt(z1.t[:], wre, tb.t[:, 3], ALU.mult, RX, [z1])
                            tt(z2.t[:], wim, tb.t[:, 2], ALU.mult, RX, [z2])
                            tt(sim, z1.t[:], z2.t[:], ALU.add, RX, [Sp.bs[sb_]])
                            evac("act", xb.t[:, sb_], Sp.t[:, sb_], [Sp.bs[sb_]], [xb.bs[sb_]])
                        for gl in range(8):
                            xiT, ysb, wq, sgm = xiTL[gl % 2], ysbL[gl % 2], wqL[gl % 2], sgmL[gl % 2]
                            p4 = nps()

                            def mX():
                                for part in range(2):
                                    for sb_ in range(4):
                                        o_ = p4.t[0:64, part * 256 + sb_ * 64: part * 256 + (sb_ + 1) * 64]
                                        ins = nc.tensor.matmul(o_, lhsT=xb.t[:, sb_, gl, part * 64:(part + 1) * 64],
                                                               rhs=ssb.t[:, 0, :], start=True, stop=(sb_ == 0))
                                        if sb_ > 0:
                                            ins = nc.tensor.matmul(o_, lhsT=xb.t[:, sb_ - 1, gl, part * 64:(part + 1) * 64],
                                                                   rhs=ssb.t[:, 1, :], start=False, stop=True)
                                return ins
                            kb.op("pe", mX, reads=[xb, ssb], writes=[p4])
                            evac("act", xiT.t[:], p4.t[0:64, :].rearrange("p (a b) -> p a b", a=2), [p4], [xiT])
                            p5 = nps()

                            def mY():
                                nc.tensor.matmul(p5.t[:, 0:256], lhsT=fl(Cj.t[:, 0, gl]), rhs=xiT.t[:, 0, :], start=True, stop=False)
                                nc.tensor.matmul(p5.t[:, 0:256], lhsT=fl(Cj.t[:, 1, gl]), rhs=xiT.t[:, 1, :], start=False, stop=False)
                                for t_ in range(4):
                                    ins = nc.tensor.matmul(p5.t[:, t_ * 64:(t_ + 1) * 64], lhsT=Tg.t[:, gl, :],
                                                           rhs=U8.t[:, gl, t_ * 128:t_ * 128 + 64], start=False, stop=(t_ == 3))
                                return ins
                            kb.op("pe", mY, reads=[Cj, xiT, Tg, U8], writes=[p5])
                            kb.op("act", lambda: nc.scalar.copy(out=ysb.t[:], in_=p5.t[:, 0:256]), reads=[p5], writes=[ysb])
                            kb.op("act", lambda: nc.scalar.activation(out=wq.t[:], in_=p5.t[:, 0:256], func=AF.Square), reads=[p5], writes=[wq])
                            kb.op("dve", lambda: nc.vector.tensor_scalar(out=wq.t[:], in0=wq.t[:], scalar1=0.044715, scalar2=1.0,
                                                                         op0=ALU.mult, op1=ALU.add), reads=[wq], writes=[wq])
                            tt(wq.t[:], wq.t[:], ysb.t[:], ALU.mult, [wq, ysb], [wq])
                            kb.op("act", lambda: nc.scalar.activation(out=sgm.t[:], in_=wq.t[:], func=AF.Sigmoid, scale=1.5957691216),
                                  reads=[wq], writes=[sgm])
                            tt(Gs.t[:, gl, :], ysb.t[:], sgm.t[:], ALU.mult, [ysb, sgm], [Gs])
                        gtv = gT.t[:, bt, :].rearrange("p (c j) -> p c j", j=8)
                        for j in range(8):
                            p_ = nps()

                            def mG():
                                for gg in range(8):
                                    ins = nc.tensor.matmul(p_.t[:, 0:256], lhsT=sel.t[:, j * 8 + gg, :], rhs=Gs.t[:, gg, :],
                                                           start=(gg == 0), stop=(gg == 7))
                                return ins
                            kb.op("pe", mG, reads=[sel, Gs], writes=[p_])
                            evac("act", gtv[:, :, j], p_.t[:, 0:256], [p_], [gT.bs[bt]])
                kb.barrier()
                pb.close()
                with contextlib.ExitStack() as pg:
                    gw = [kb.sb(f"gw{i}", [128, 8, 256], BF16, pg) for i in range(2)]
                    bcol = kb.sb("bcol", [128, 8], F32, pg)
                    sgl = [kb.sb(f"sgl{i}", [128, 512], F32, pg) for i in range(2)]
                    gost = [kb.sb(f"gost{i}", [128, 512], BF16, pg) for i in range(2)]
                    with nc.allow_non_contiguous_dma(reason="tiny"):
                        kb.dma("sp", bcol.t[:], I["ssm_glu_b"].rearrange("(t p) -> p t", p=128), writes=[bcol])
                    gv = I["ssm_glu_w"].rearrange("(k p) n -> p k n", p=128)
                    oi = 0
                    for sl in range(4):
                        w2 = gw[sl % 2]
                        kb.dma("pool", w2.t[:], gv[:, :, sl * 256:(sl + 1) * 256], writes=[w2])
                        for j in range(2):
                            m = sl * 2 + j
                            for hf in range(4):
                                p_ = nps()

                                def mm():
                                    for k in range(8):
                                        ins = nc.tensor.matmul(p_.t[:], lhsT=w2.t[:, k, j * 128:(j + 1) * 128],
                                                               rhs=gT.t[:, k, hf * 512:(hf + 1) * 512], start=(k == 0), stop=(k == 7))
                                    return ins
                                kb.op("pe", mm, reads=[w2, gT], writes=[p_])
                                s_ = sgl[oi % 2]
                                o_ = gost[oi % 2]
                                oi += 1
                                kb.op("act", lambda: nc.scalar.activation(out=s_.t[:], in_=p_.t[:], func=AF.Sigmoid, bias=bcol.t[:, m:m + 1]),
                                      reads=[p_, bcol], writes=[s_])
                                tt(o_.t[:], gT.t[:, m, hf * 512:(hf + 1) * 512], s_.t[:], ALU.mult, [gT.bs[m], s_], [o_])
                                kb.dma("sp", mixT[8 + m, :, hf * 512:(hf + 1) * 512], o_.t[:], reads=[o_])
                    kb.barrier()

        kb.barrier()
        if "A" in PHASES:
            phase_A()
        if "S" in PHASES:
            phase_S()
        if "T" in PHASES:
            phase_T()
        if "C" in PHASES:
            phase_C()
        kb.barrier()
    return nc


_NC = None


def _get_nc():
    global _NC
    if _NC is None:
        _NC = build()
    return _NC


def make_in_maps(inputs):
    x = np.asarray(inputs["x"], dtype=np.float32)
    shared = {}
    for n, s in IN_SPECS:
        if n == "xl":
            continue
        a = np.asarray(inputs[n], dtype=np.float32)
        shared[n] = np.ascontiguousarray(a.reshape(s))
    consts = [host_consts(r) for r in range(2)]
    in_maps = []
    for c in range(8):
        b, r = c // 2, c % 2
        xb = x[b].reshape(4, 4, 2, 128, D)
        own = xb[:, :, r]
        par = xb[:, :, 1 - r]
        xl = np.concatenate([own.reshape(4, 512, D), par.reshape(4, 512, D)], axis=1).reshape(4096, D)
        m = dict(shared)
        m["xl"] = np.ascontiguousarray(xl)
        for n, s, dt in CONST_SPECS:
            m["c_" + n] = consts[r][n]
        in_maps.append(m)
    return in_maps


def kernel(**inputs):
    nc = _get_nc()
    in_maps = make_in_maps(inputs)
    res = run_bass_kernel_spmd(nc, in_maps, core_ids=list(range(8)))
    y = np.zeros((4, 4096, D), np.float32)
    yv = y.reshape(4, 4, 4, 2, 128, D)
    for c in range(8):
        b, r = c // 2, c % 2
        o = np.asarray(res.results[c]["out"], dtype=np.float32).reshape(4, 4, 128, D)
        yv[b, :, :, r] = o
    if DBG:
        kernel.last = res
    return y
```
